# Optimizing a Trainium2 kernel written in Bass

```python
import jax
import jax.numpy as jnp
from jax import lax
import numpy as np

D_MODEL = 2048
BATCH = 16
SEQ = 2048
DEPTH = 1
DEC_BATCH = 128
DEC_SEQ = 1
PAST_LEN = 16384
PAGE_SIZE = 128

HEAD_DIM = 64
N_HEADS = 16
N_KV_HEADS = 4
GROUP = N_HEADS // N_KV_HEADS
WINDOW = 128
BLOCK = WINDOW
ROT_DIM = HEAD_DIM // 4
ROPE_THETA = 500000.0
ATTN_SCALE = HEAD_DIM ** -0.5
Q_DIM = N_HEADS * HEAD_DIM
KV_DIM = N_KV_HEADS * HEAD_DIM
CONV_CH = D_MODEL - Q_DIM
CONV_W = 31
MIX_WIDTH = Q_DIM + CONV_CH
IN_DIM = Q_DIM + 2 * KV_DIM + 2 * CONV_CH
D_FF = 4 * D_MODEL
EPS = 1e-5

kernel_name = 'hymba_swa_sink_conformer_conv_sqrelu_step'


def _rms_norm(x, g):
    xf = x.astype(jnp.float32)
    y = xf * lax.rsqrt(jnp.mean(xf * xf, axis=-1, keepdims=True) + EPS)
    return (y * g.astype(jnp.float32)).astype(x.dtype)


def _partial_rope(x, pos):
    half = ROT_DIM // 2
    inv_freq = jnp.power(jnp.float32(ROPE_THETA), -jnp.arange(half, dtype=jnp.float32) * 2.0 / ROT_DIM)
    ang = pos.astype(jnp.float32)[:, None] * inv_freq[None, :]
    cos = jnp.cos(ang)[None, :, None, :]
    sin = jnp.sin(ang)[None, :, None, :]
    xr = x[..., :ROT_DIM].astype(jnp.float32)
    x1, x2 = xr[..., :half], xr[..., half:]
    rot = jnp.concatenate([x1 * cos - x2 * sin, x2 * cos + x1 * sin], axis=-1).astype(x.dtype)
    return jnp.concatenate([rot, x[..., ROT_DIM:]], axis=-1)


def _project_in(hn, w_in, b_in, pos):
    n, t, _ = hn.shape
    z = hn @ w_in + b_in
    o1, o2, o3, o4 = Q_DIM, Q_DIM + KV_DIM, Q_DIM + 2 * KV_DIM, Q_DIM + 2 * KV_DIM + CONV_CH
    q = z[..., :o1].reshape(n, t, N_HEADS, HEAD_DIM)
    k = z[..., o1:o2].reshape(n, t, N_KV_HEADS, HEAD_DIM)
    v = z[..., o2:o3].reshape(n, t, N_KV_HEADS, HEAD_DIM)
    u = z[..., o3:o4] * jax.nn.sigmoid(z[..., o4:])
    return _partial_rope(q, pos), _partial_rope(k, pos), v, u


def _sink_softmax(s, mask, sink):
    s = jnp.where(mask, s, -jnp.inf)
    m = jnp.maximum(jnp.max(s, axis=-1, keepdims=True), sink)
    p = jnp.exp(s - m)
    return p / (jnp.sum(p, axis=-1, keepdims=True) + jnp.exp(sink - m))


def _banded_window_attention(q, k, v, sinks):
    n, t = q.shape[:2]
    nb = t // BLOCK
    qb = q.reshape(n, nb, BLOCK, N_KV_HEADS, GROUP, HEAD_DIM)
    kb = k.reshape(n, nb, BLOCK, N_KV_HEADS, HEAD_DIM)
    vb = v.reshape(n, nb, BLOCK, N_KV_HEADS, HEAD_DIM)

    def with_prev(xb):
        prev = jnp.concatenate([jnp.zeros_like(xb[:, :1]), xb[:, :-1]], axis=1)
        return jnp.concatenate([prev, xb], axis=2)

    kk, vv = with_prev(kb), with_prev(vb)
    s = jnp.einsum('bnqhgd,bnshd->bnhgqs', qb, kk, preferred_element_type=jnp.float32) * ATTN_SCALE
    i = jnp.arange(BLOCK)[:, None]
    j = jnp.arange(2 * BLOCK)[None, :]
    rel = i + BLOCK - j
    band = (rel >= 0) & (rel < WINDOW)
    has_prev = (jnp.arange(nb) > 0)[:, None, None] | (j >= BLOCK)[None]
    mask = (band[None] & has_prev)[None, :, None, None]
    sink = sinks.astype(jnp.float32).reshape(1, 1, N_KV_HEADS, GROUP, 1, 1)
    p = _sink_softmax(s, mask, sink)
    o = jnp.einsum('bnhgqs,bnshd->bnqhgd', p.astype(v.dtype), vv)
    return o.reshape(n, t, Q_DIM)


def _window_cache_attention(q, k, v, buf_k, buf_v, sinks):
    n, t = q.shape[:2]
    wb = buf_k.shape[1]
    kk = jnp.concatenate([buf_k.astype(k.dtype), k], axis=1)
    vv = jnp.concatenate([buf_v.astype(v.dtype), v], axis=1)
    qg = q.reshape(n, t, N_KV_HEADS, GROUP, HEAD_DIM)
    s = jnp.einsum('bqhgd,bshd->bhgqs', qg, kk, preferred_element_type=jnp.float32) * ATTN_SCALE
    q_pos = PAST_LEN + jnp.arange(t)
    k_pos = PAST_LEN - wb + jnp.arange(wb + t)
    rel = q_pos[:, None] - k_pos[None, :]
    mask = ((rel >= 0) & (rel < WINDOW))[None, None, None]
    sink = sinks.astype(jnp.float32).reshape(1, N_KV_HEADS, GROUP, 1, 1)
    p = _sink_softmax(s, mask, sink)
    o = jnp.einsum('bhgqs,bshd->bqhgd', p.astype(vv.dtype), vv)
    return o.reshape(n, t, Q_DIM), kk[:, -wb:], vv[:, -wb:]


def _conv_branch(u_ext, conv_w, conv_b, ln_g, ln_b):
    y = lax.conv_general_dilated(
        u_ext, conv_w[:, None, :].astype(u_ext.dtype), window_strides=(1,), padding='VALID',
        dimension_numbers=('NWC', 'WIO', 'NWC'), feature_group_count=u_ext.shape[-1])
    y = (y + conv_b).astype(jnp.float32)
    mu = jnp.mean(y, axis=-1, keepdims=True)
    yc = y - mu
    var = jnp.mean(yc * yc, axis=-1, keepdims=True)
    yn = yc * lax.rsqrt(var + EPS) * ln_g.astype(jnp.float32) + ln_b.astype(jnp.float32)
    return (yn * jax.nn.sigmoid(yn)).astype(u_ext.dtype)


def _merge_and_mlp(h, attn_o, conv_o, w_out, b_out, norm2_g, w_up, w_down):
    h = h + jnp.concatenate([attn_o, conv_o], axis=-1) @ w_out + b_out
    hn = _rms_norm(h, norm2_g)
    return h + jnp.square(jax.nn.relu(hn @ w_up)) @ w_down


def setup_inputs(seed: int = 0) -> dict:
    key = jax.random.key(seed)
    ks = jax.random.split(key, 20)
    f32 = jnp.float32
    wb = min(WINDOW, PAST_LEN)
    nrm = lambda k, shape, scale: jax.random.normal(k, shape, f32) * scale
    return {
        'x_prompt': nrm(ks[0], (BATCH, SEQ, D_MODEL), 1.0),
        'x_sample': nrm(ks[1], (DEC_BATCH, DEC_SEQ, D_MODEL), 1.0),
        'cache_k': nrm(ks[2], (DEPTH, DEC_BATCH, wb, N_KV_HEADS, HEAD_DIM), 1.0),
        'cache_v': nrm(ks[3], (DEPTH, DEC_BATCH, wb, N_KV_HEADS, HEAD_DIM), 1.0),
        'cache_conv': nrm(ks[4], (DEPTH, DEC_BATCH, CONV_W - 1, CONV_CH), 0.5),
        'norm1_g': 1.0 + nrm(ks[5], (DEPTH, D_MODEL), 0.02),
        'w_in': nrm(ks[6], (DEPTH, D_MODEL, IN_DIM), D_MODEL ** -0.5),
        'b_in': nrm(ks[7], (DEPTH, IN_DIM), 0.02),
        'attn_sinks': nrm(ks[8], (DEPTH, N_HEADS), 0.5),
        'conv_w': nrm(ks[9], (DEPTH, CONV_W, CONV_CH), CONV_W ** -0.5),
        'conv_b': nrm(ks[10], (DEPTH, CONV_CH), 0.02),
        'conv_ln_g': 1.0 + nrm(ks[11], (DEPTH, CONV_CH), 0.02),
        'conv_ln_b': nrm(ks[12], (DEPTH, CONV_CH), 0.02),
        'w_out': nrm(ks[13], (DEPTH, MIX_WIDTH, D_MODEL), MIX_WIDTH ** -0.5),
        'b_out': nrm(ks[14], (DEPTH, D_MODEL), 0.02),
        'norm2_g': 1.0 + nrm(ks[15], (DEPTH, D_MODEL), 0.02),
        'w_up': nrm(ks[16], (DEPTH, D_MODEL, D_FF), D_MODEL ** -0.5),
        'w_down': nrm(ks[17], (DEPTH, D_FF, D_MODEL), D_FF ** -0.5),
        'final_norm_g': 1.0 + nrm(ks[18], (D_MODEL,), 0.02),
    }


def reference(x_prompt, x_sample, cache_k, cache_v, cache_conv, norm1_g, w_in, b_in, attn_sinks,
              conv_w, conv_b, conv_ln_g, conv_ln_b, w_out, b_out, norm2_g, w_up, w_down, final_norm_g):
    t_p = x_prompt.shape[1]
    t_s = x_sample.shape[1]
    pos_p = jnp.arange(t_p, dtype=jnp.int32)
    pos_s = PAST_LEN + jnp.arange(t_s, dtype=jnp.int32)
    wp = min(WINDOW, t_p)
    hp, hs = x_prompt, x_sample
    pk, pv, pc, sk, sv, sc = [], [], [], [], [], []
    for l in range(DEPTH):
        q, k, v, u = _project_in(_rms_norm(hp, norm1_g[l]), w_in[l], b_in[l], pos_p)
        a_o = _banded_window_attention(q, k, v, attn_sinks[l])
        c_o = _conv_branch(jnp.pad(u, ((0, 0), (CONV_W - 1, 0), (0, 0))),
                           conv_w[l], conv_b[l], conv_ln_g[l], conv_ln_b[l])
        hp = _merge_and_mlp(hp, a_o, c_o, w_out[l], b_out[l], norm2_g[l], w_up[l], w_down[l])
        pk.append(k[:, -wp:])
        pv.append(v[:, -wp:])
        pc.append(u[:, -(CONV_W - 1):])
        q, k, v, u = _project_in(_rms_norm(hs, norm1_g[l]), w_in[l], b_in[l], pos_s)
        a_o, nk, nv = _window_cache_attention(q, k, v, cache_k[l], cache_v[l], attn_sinks[l])
        u_ext = jnp.concatenate([cache_conv[l].astype(u.dtype), u], axis=1)
        c_o = _conv_branch(u_ext, conv_w[l], conv_b[l], conv_ln_g[l], conv_ln_b[l])
        hs = _merge_and_mlp(hs, a_o, c_o, w_out[l], b_out[l], norm2_g[l], w_up[l], w_down[l])
        sk.append(nk)
        sv.append(nv)
        sc.append(u_ext[:, -(CONV_W - 1):])
    y_prompt = _rms_norm(hp, final_norm_g)
    y_sample = _rms_norm(hs, final_norm_g)
    return (y_prompt, y_sample, jnp.stack(pk), jnp.stack(pv), jnp.stack(pc),
            jnp.stack(sk), jnp.stack(sv), jnp.stack(sc))
```

```python
import numpy as np
from contextlib import ExitStack
import concourse.bass as bass
import concourse.mybir as mybir
from concourse.bass_utils import run_bass_kernel_spmd

F32 = mybir.dt.float32
BF16 = mybir.dt.bfloat16
AF = mybir.ActivationFunctionType
ALU = mybir.AluOpType
AX = mybir.AxisListType

D = 2048
NCORE = 8
TOK = 4096
TT = 512
NTILE = TOK // TT
SEQ = 2048
NS = 16
DFF = 8192
EPS = 1e-5
NSLOT = 3

C_ID = 0
C_PM = 128
C_ONE = 256
C_MK = 384
C_MK0 = 640
C_G1 = 896
C_G2 = 912
C_CB = 928
C_LG = 936
C_LB = 944
C_CW = 952
C_BFM = 1200
C_SK = 1232
C_EPS = 1248
C_END = 1249


def head_of(c, b):
    g = 2 * (c // 4) + b
    return 4 * g + (c % 4)


class Sem:
    def __init__(self, h):
        self.h = h
        self.v = 0


class Eng:
    def __init__(self, h, semh):
        self.h = h
        self.sem = Sem(semh)
        self.waited = {}

    def wait(self, toks):
        for t in toks:
            if t is None:
                continue
            sm, v = t
            if self.waited.get(id(sm), 0) >= v:
                continue
            self.h.wait_ge(sm.h, v)
            self.waited[id(sm)] = v

    def op(self, ins):
        self.sem.v += 1
        ins.then_inc(self.sem.h, 1)
        return (self.sem, self.sem.v)


def build_nc():
    nc = bass.Bass("TRN2", target_bir_lowering=False)

    def din(name, shape):
        return nc.dram_tensor(name, list(shape), F32, kind="ExternalInput").ap()

    def dout(name, shape):
        return nc.dram_tensor(name, list(shape), F32, kind="ExternalOutput").ap()

    xp = din("xp", [TOK, D])
    xs_d = din("xs", [NS, D])
    ck = din("ck", [NS, 128, 256])
    cv = din("cv", [NS, 128, 256])
    cc = din("cc", [NS, 30, 1024])
    wt = din("wt", [86, 128, 4096])
    cst = din("cst", [128, C_END])
    csp = din("csp", [128, 2, SEQ])
    css = din("css", [128, 2, NS])
    gf = din("gf", [D])
    bo = din("bo", [D])
    bv = din("bv", [256])

    wsc = nc.dram_tensor("wsc", [86, 128, 4096], BF16, kind="Internal").ap()
    xbs = nc.dram_tensor("xbs", [2, TT, D], F32, kind="Internal").ap()

    y_p = dout("y_p", [TOK, D])
    y_s = dout("y_s", [NS, D])
    o_pk = dout("o_pk", [2, 128, 256])
    o_pv = dout("o_pv", [2, 128, 256])
    o_pc = dout("o_pc", [2, 30, 1024])
    o_sk = dout("o_sk", [NS, 128, 256])
    o_sv = dout("o_sv", [NS, 128, 256])
    o_sc = dout("o_sc", [NS, 30, 1024])

    with ExitStack() as es:
        def sb(name, shape, dt=F32):
            return es.enter_context(nc.sbuf_tensor(name, list(shape), dt))

        def newsem(name):
            return es.enter_context(nc.semaphore(name))

        cs = sb("cs", [128, C_END])
        xsb = [sb("xsb0", [128, D]), sb("xsb1", [128, D])]
        sqj = sb("sqj", [128, D], BF16)
        xnT = sb("xnT", [128, 16 * TT], BF16)
        hres_t = sb("hres_t", [128, 8192])
        abuf = sb("abuf", [128, 12272], BF16)
        ych = sb("ych", [128, 4096])
        mixT = sb("mixT", [128, 16 * TT], BF16)
        wr = [sb("wr%d" % i, [128, 4096], BF16) for i in range(NSLOT)]
        cst_t = sb("cst_t", [128, 2, TT])
        gfb = sb("gfb", [128, D])
        bvb = sb("bvb", [128, 256])
        zq = sb("zq", [128, TT])
        zqb2 = [sb("zqb0", [128, TT], BF16), sb("zqb1", [128, TT], BF16)]
        zr = [sb("zr0", [128, TT]), sb("zr1", [128, TT])]
        sgt = sb("sgt", [128, 2 * TT])
        sg = [sgt[:, 0:TT], sgt[:, TT:2 * TT]]
        ucout = sgt[0:32, :]
        v32 = [sb("v320", [128, 256]), sb("v321", [128, 256])]
        NPB = 3
        pb = [sb("pb%d" % i, [128, 260]) for i in range(NPB)]
        NPN = 5
        pn = [sb("pn%d" % i, [128, 256], BF16) for i in range(NPN)]
        NPT = 3
        pts = [sb("pts%d" % i, [128, 256], BF16) for i in range(NPT)]
        dg = [sb("dg%d" % i, [128, 128], BF16) for i in range(6)]
        maskb = sb("maskb", [128, 2, 256], BF16)
        ulast = sb("ulast", [128, 8, 30])
        sm_mx = sb("sm_mx", [128, 64])
        sm_nb = sb("sm_nb", [128, 64])
        sm_es = sb("sm_es", [128, 64])
        sm_rs = sb("sm_rs", [128, 64])
        sm_rd = sb("sm_rd", [128, 64])
        sk8 = sb("sk8", [128, 16])
        skh = sb("skh", [128, 16], BF16)
        skl = sb("skl", [128, 16], BF16)
        ysq1 = sb("ysq0", [128, TT])
        ysq = [ysq1, None]
        ysq[1] = zq
        mu = sb("mu", [128, TT])
        kout = mu[:, 0:256]
        rs_ln = sb("rs_ln", [128, TT])
        rl = [sb("rl0", [128, TT]), sb("rl1", [128, TT])]
        ss = sb("ss", [128, 16])
        rstd = sb("rstd", [128, 16])
        khalo = sb("khalo", [128, 2, 128], BF16)
        vhalo = sb("vhalo", [128, 256], BF16)
        uhalo = sb("uhalo", [128, 8, 30], BF16)
        xnT_s = sb("xnT_s", [128, 16, NS], BF16)
        mixT_s = sb("mixT_s", [128, 16, NS], BF16)
        h2T_s = sb("h2T_s", [128, 16, NS], BF16)
        yc_s = sb("yc_s", [128, 8, NS])
        qT_s = sb("qT_s", [128, 8, NS], BF16)
        krot_s = sb("krot_s", [128, 2, NS])
        uext_s = sb("uext_s", [128, 8, 30 + NS])
        st_sb = zq[:, 0:256]
        pn_s = zr[0][:, 0:256].rearrange("p (h k) -> p h k", h=2)
        pt_s = zr[1][:, 0:256]
        ktm_s = rl[0][0:NS, 0:256]

        R = hres_t
        xnT_v = xnT[:].rearrange("p (k t) -> p k t", k=16)
        mixT_v = mixT[:].rearrange("p (k t) -> p k t", k=16)
        hres = hres_t[:].rearrange("p (g d) -> p g d", g=4)
        qT = abuf[:, 0:4096].rearrange("p (k t) -> p k t", k=8)
        kTa = abuf[:, 4096:4096 + 1280].rearrange("p (k t) -> p k t", k=2)
        kTb = abuf[:, 10992:10992 + 1280].rearrange("p (k t) -> p k t", k=2)
        Vt = abuf[:, 5376:5376 + 1280].rearrange("p (g d) -> p g d", g=5)
        uext = abuf[:, 6656:6656 + 8 * 542].rearrange("p (j t) -> p j t", j=8)
        yc = ych[:].rearrange("p (j t) -> p j t", j=8)
        h2T = ych[:].bitcast(BF16).rearrange("p (k t) -> p k t", k=16)
        Kc = ych[:].rearrange("p (s d) -> p s d", s=NS)
        Vc = mixT[:].bitcast(F32).rearrange("p (s d) -> p s d", s=NS)
        ccT = xnT[:].bitcast(F32)[:, 0:8 * NS * 30].rearrange("p (j s w) -> p j s w", j=8, s=NS)
        ccst = R[0:120, 2048:2048 + 4096].rearrange("p (a c) -> p a c", a=4)
        hres_s = R[0:NS, 0:2048]
        knT = xsb[1][:].bitcast(BF16).rearrange("p (c s k) -> p c s k", c=2, s=NS)
        utm_s = ucout[0:NS, :]
        prod_h = R[:, 6144:6144 + 4 * NS * 30].rearrange("p (j s w) -> p j s w", j=4, s=NS)
        prod_u = R[:, 6144:6144 + 8 * NS].rearrange("p (j s) -> p j s", j=8)

        ident = cs[:, C_ID:C_ID + 128]
        pmat = sb("pmat", [128, 128], BF16)
        ones = cs[:, C_ONE:C_ONE + 128]
        identb = sb("identb", [128, 128], BF16)

        banks = [es.enter_context(nc.psum_tensor("bk%d" % i, [128, 512], F32)) for i in range(8)]
        bank_free = [None] * 8
        bank_ptr = [0]

        PE = Eng(nc.tensor, newsem("s_pe"))
        ACT = Eng(nc.scalar, newsem("s_act"))
        DVE = Eng(nc.vector, newsem("s_dve"))
        POOL = Eng(nc.gpsimd, newsem("s_pool"))
        SP = Eng(nc.sync, newsem("s_sp"))
        sem_w = [Sem(newsem("s_w%d" % i)) for i in range(NSLOT)]
        sem_x = [Sem(newsem("s_x0")), Sem(newsem("s_x1"))]
        sem_c = Sem(newsem("s_c"))
        sem_hg = [Sem(newsem("s_h%d" % i)) for i in range(4)]
        sem_bg = [Sem(newsem("s_b%d" % i)) for i in range(4)]
        sem_pb = Sem(newsem("s_pb"))
        sem_sk = Sem(newsem("s_sk"))
        sem_xo = [Sem(newsem("s_xo%d" % i)) for i in range(2)]
        sem_o = Sem(newsem("s_o"))
        sem_t = Sem(newsem("s_t"))
        sem_m = Sem(newsem("s_m"))
        sem_ws = [Sem(newsem("s_ws%d" % i)) for i in range(NSLOT)]

        def dma(eng, sem, out, in_, slow=False, accum=None):
            kw = {}
            if slow:
                kw["allow_slow_non_contiguous"] = True
            if accum is not None:
                kw["accum_op"] = accum
            eng.h.dma_start(out=out, in_=in_, **kw).then_inc(sem.h, 16)
            sem.v += 16
            return (sem, sem.v)

        def acquire(n=1):
            if bank_ptr[0] + n > 4:
                bank_ptr[0] = 0
            idx = list(range(bank_ptr[0], bank_ptr[0] + n))
            bank_ptr[0] = (bank_ptr[0] + n) % 4
            PE.wait([bank_free[i] for i in idx])
            return idx

        one_pass = [wt[t, :, :] for t in range(86)]
        NPW = len(one_pass)
        ID_B = list(range(0, 14))
        ID_E = list(range(14, 22))
        ID_G = list(range(22, 86))
        order = ID_B + ID_E
        for i in range(NTILE):
            if i + 1 < NTILE:
                order += ID_B
            order += ID_G
            if i + 1 < NTILE:
                order += ID_E
        order += ID_B + ID_E + ID_G
        seen = set()
        first_use = []
        for t in order:
            first_use.append(t not in seen)
            seen.add(t)
        w_issued = [0]
        w_load_tok = {}
        w_free_tok = {}
        w_next = [0]
        w_sc_tok = {}
        w_slot_sc = {}

        def w_prefetch(upto):
            while w_issued[0] <= upto and w_issued[0] < len(order):
                i = w_issued[0]
                tid = order[i]
                s = i % NSLOT
                prev = i - NSLOT
                if first_use[i]:
                    if prev >= 0:
                        POOL.wait([w_free_tok[prev], w_slot_sc.get(prev)])
                    src = one_pass[tid]
                    dma(POOL, sem_w[s], wr[s][:, 0:2048], src[:, 0:2048])
                    w_load_tok[i] = dma(POOL, sem_w[s], wr[s][:, 2048:4096], src[:, 2048:4096])
                else:
                    if prev >= 0:
                        SP.wait([w_free_tok[prev], w_slot_sc.get(prev)])
                    SP.wait([w_sc_tok[tid]])
                    w_load_tok[i] = dma(SP, sem_w[s], wr[s][:], wsc[tid, :, :])
                w_issued[0] += 1

        def w_get():
            i = w_next[0]
            w_next[0] += 1
            w_prefetch(i + NSLOT - 1)
            PE.wait([w_load_tok[i]])
            if first_use[i]:
                SP.wait([w_load_tok[i]])
                w_sc_tok[order[i]] = dma(SP, sem_ws[i % NSLOT], wsc[order[i], :, :], wr[i % NSLOT][:])
                w_slot_sc[i] = w_sc_tok[order[i]]
            return i, wr[i % NSLOT][:].rearrange("p (a b) -> p a b", a=16)

        def w_done(i, tok):
            w_free_tok[i] = tok

        t_c = dma(SP, sem_c, cs[:], cst[:, :])
        t_c = dma(SP, sem_c, bvb[:], bv.partition_broadcast(128))
        DVE.wait([t_c])
        DVE.op(nc.vector.tensor_copy(pmat[:], cs[:, C_PM:C_PM + 128]))
        DVE.op(nc.vector.tensor_copy(identb[:], ident))
        t_k8 = DVE.op(nc.vector.tensor_scalar(sk8[:], cs[:, C_SK:C_SK + 16], 8.0, None, ALU.mult))
        DVE.wait([t_k8])
        t_k8 = DVE.op(nc.vector.tensor_copy(skh[:], sk8[:]))
        DVE.wait([t_k8])
        t_k8 = DVE.op(nc.vector.tensor_tensor(out=skl[:], in0=sk8[:], in1=skh[:], op=ALU.subtract))
        DVE.op(nc.vector.tensor_scalar(maskb[:, 0, :], cs[:, C_MK:C_MK + 256], -1.0, 30000.0, ALU.add, ALU.mult))
        DVE.op(nc.vector.tensor_scalar(maskb[:, 1, :], cs[:, C_MK0:C_MK0 + 256], -1.0, 30000.0, ALU.add, ALU.mult))
        DVE.op(nc.vector.memset(kTa[:], 0.0))
        DVE.op(nc.vector.memset(kTb[:], 0.0))
        DVE.op(nc.vector.memset(khalo[:], 0.0))
        DVE.op(nc.vector.memset(vhalo[:], 0.0))
        t_init = DVE.op(nc.vector.memset(uhalo[:], 0.0))
        for e in (PE, ACT, POOL):
            e.wait([t_c, t_init])

        early_tok = [None]
        state = {"hres_rd": None, "xn_rd": None, "ych_rd": None, "mix_rd": None, "abuf_rd": None}
        tm_last_win = [None]
        dg_ctr = [0]
        dg_free = [None] * 6
        xb_free_ref = [None, None]
        xb_free = xb_free_ref
        rope_zb_free = [None, None]
        rope_zr_free = [None, None]
        xb_st = [None, None]

        class Ctx:
            pass

        def make_ctx(ti):
            c = Ctx()
            c.ti = ti
            c.sample = ti == NTILE
            if c.sample:
                c.tt, c.gs, c.ng = NS, NS, 1
                c.xnT, c.mixT, c.h2T, c.yc, c.qT = xnT_s[:], mixT_s[:], h2T_s[:], yc_s[:], qT_s[:]
                c.uext = uext_s[:]
                c.hres = hres_s.rearrange("p (g d) -> p g d", g=1)
                c.first = False
                c.last = False
                c.seq = 0
                c.pos0 = 0
                c.xsrc = xs_d
                c.ydst = y_s
                c.r0 = 0
            else:
                c.tt, c.gs, c.ng = TT, 128, 4
                c.xnT, c.mixT, c.h2T, c.yc, c.qT = xnT_v, mixT_v, h2T, yc, qT
                c.uext = uext
                c.hres = hres
                c.first = (ti % 4) == 0
                c.last = (ti % 4) == 3
                c.seq = ti // 4
                c.pos0 = (ti % 4) * TT
                c.xsrc = xp
                c.ydst = y_p
                c.r0 = ti * TT
            return c

        def norm_pre(c, g, src_fn):
            gs = c.gs
            xb = xsb[g % 2]
            t_src, src = src_fn(g, xb)
            ACT.wait([t_src])
            ACT.op(nc.scalar.activation(out=sqj[0:gs, :], in_=src, func=AF.Square,
                                        accum_out=ss[0:gs, g:g + 1]))
            t1 = ACT.op(nc.scalar.activation(out=ss[0:gs, 8 + g:9 + g], in_=ss[0:gs, g:g + 1], func=AF.Sqrt,
                                             scale=1.0 / D, bias=cs[0:gs, C_EPS:C_EPS + 1]))
            DVE.wait([t1])
            t2 = DVE.op(nc.vector.reciprocal(rstd[0:gs, g:g + 1], ss[0:gs, 8 + g:9 + g]))
            ACT.wait([t2])
            return ACT.op(nc.scalar.activation(out=xb[0:gs, :], in_=src, func=AF.Copy,
                                               scale=rstd[0:gs, g:g + 1]))

        def norm_post(c, g, t3, gcol, dstT, extra_wait):
            gs = c.gs
            xb = xsb[g % 2]
            toks = []
            PE.wait([t3])
            for k0 in range(0, 16, 4):
                (b,) = acquire(1)
                bv_ = banks[b][:].rearrange("p (a t) -> p a t", a=4)
                for a in range(4):
                    kc = k0 + a
                    ins = nc.tensor.transpose(bv_[:, a, 0:gs], xb[0:gs, kc * 128:(kc + 1) * 128],
                                              ident[0:gs, 0:gs])
                t4 = PE.op(ins)
                DVE.wait([t4, extra_wait])
                t5 = DVE.op(nc.vector.tensor_tensor(
                    out=dstT[:, k0:k0 + 4, g * gs:(g + 1) * gs], in0=bv_[:, :, 0:gs],
                    in1=cs[:, gcol + k0:gcol + k0 + 4].unsqueeze(2).to_broadcast([128, 4, gs]),
                    op=ALU.mult))
                bank_free[b] = t5
                toks.append(t5)
            xb_free[g % 2] = t4
            return toks

        def norm_T(c, src_fn, gcol, dstT, extra_wait):
            toks = []
            for g in range(c.ng):
                t3 = norm_pre(c, g, src_fn)
                toks += norm_post(c, g, t3, gcol, dstT, extra_wait)
            return toks

        def fm_chunk(c, wv, j, gen=None):
            (b,) = acquire(1)
            for kc in range(16):
                ins = nc.tensor.matmul(banks[b][:, 0:c.tt], wv[:, kc, j * 128:(j + 1) * 128], c.xnT[:, kc, 0:c.tt],
                                       start=(kc == 0), stop=(kc == 15))
            return b, PE.op(ins)

        def ph_A_gen(c):
            gs = c.gs
            SP.wait([state["abuf_rd"]])
            if c.sample:
                c.t_cs = dma(SP, sem_t, cst_t[:, :, 0:c.tt], css[:, :, :])
            else:
                c.t_cs = dma(SP, sem_t, cst_t[:, :, :], csp[:, :, c.pos0:c.pos0 + TT])

            def src_x(g, xb):
                SP.wait([xb_free[g % 2], xb_st[g % 2]])
                return dma(SP, sem_x[g % 2], xb[0:gs, :], c.xsrc[c.r0 + g * gs:c.r0 + (g + 1) * gs, :]), xb[0:gs, :]

            if not c.sample:
                DVE.wait([state["abuf_rd"]])
                if c.first:
                    DVE.op(nc.vector.memset(kTa[0:64, :, 0:128], 0.0))
                    DVE.op(nc.vector.memset(kTb[64:128, :, 0:128], 0.0))
                    DVE.op(nc.vector.memset(Vt[:, 0, :], 0.0))
                    c.t_h = DVE.op(nc.vector.memset(uext[:, :, 0:30], 0.0))
                else:
                    DVE.op(nc.vector.tensor_copy(kTa[0:64, :, 0:128], khalo[0:64, :, :]))
                    DVE.op(nc.vector.tensor_copy(kTb[64:128, :, 0:128], khalo[64:128, :, :]))
                    DVE.op(nc.vector.tensor_copy(Vt[:, 0, :], vhalo[:]))
                    c.t_h = DVE.op(nc.vector.tensor_copy(uext[:, :, 0:30], uhalo[:]))
            else:
                c.t_h = None
            tA = []
            t3 = {}
            ng = c.ng
            order_ = []
            if ng == 1:
                order_ = [("pre", 0), ("y",), ("post", 0)]
            else:
                order_ = [("pre", 0), ("y",), ("pre", 1), ("y",), ("post", 0), ("pre", 2), ("y",),
                          ("post", 1), ("pre", 3), ("y",), ("post", 2), ("y",), ("post", 3)]
            for it in order_:
                if it[0] == "pre":
                    t3[it[1]] = norm_pre(c, it[1], src_x)
                elif it[0] == "post":
                    tA += norm_post(c, it[1], t3[it[1]], C_G1, c.xnT, state["xn_rd"])
                else:
                    yield
            c.tA = tA

        def ph_A(c):
            drain(ph_A_gen(c))

        def ph_B(c):
            ph_B1(c)
            ph_B2(c)

        def ph_B1(c):
            tt, gs, ng, sample, last = c.tt, c.gs, c.ng, c.sample, c.last
            a_ok = state["abuf_rd"]
            t_h = c.t_h
            PE.wait(c.tA)
            t_u = [None] * 8
            for j in range(8):
                wi, wv = w_get()
                b, tm = fm_chunk(c, wv, 0)
                ACT.wait([tm, state.get("ucout_rd")])
                t_sg = ACT.op(nc.scalar.activation(out=sg[j % 2][:, 0:tt], in_=banks[b][:, 0:tt], func=AF.Sigmoid,
                                                   bias=cs[:, C_BFM + 2 * j:C_BFM + 2 * j + 1], scale=1.0))
                bank_free[b] = t_sg
                b, tm = fm_chunk(c, wv, 1)
                w_done(wi, tm)
                DVE.wait([tm, t_sg, a_ok, t_h])
                t_u[j] = DVE.op(nc.vector.scalar_tensor_tensor(
                    out=c.uext[:, j, 30:30 + tt], in0=banks[b][:, 0:tt],
                    scalar=cs[:, C_BFM + 2 * j + 1:C_BFM + 2 * j + 2], in1=sg[j % 2][:, 0:tt],
                    op0=ALU.add, op1=ALU.mult))
                if last:
                    DVE.wait([state.get("ucout_rd")])
                    t_u[j] = DVE.op(nc.vector.scalar_tensor_tensor(
                        out=ulast[:, j, :], in0=banks[b][:, TT - 30:TT],
                        scalar=cs[:, C_BFM + 2 * j + 1:C_BFM + 2 * j + 2], in1=sg[j % 2][:, TT - 30:TT],
                        op0=ALU.add, op1=ALU.mult))
                bank_free[b] = t_u[j]
                ACT.wait([t_u[j]])
            c.t_u = t_u

            c.t_conv = None
            if not sample:
                t_conv = [None] * 8
                ACT.wait([state["ych_rd"]])
                for j in range(8):
                    (b,) = acquire(1)
                    PE.wait([t_u[j]])
                    for w in range(31):
                        k = dg_ctr[0] % 6
                        dg_ctr[0] += 1
                        POOL.wait([dg_free[k]])
                        t_d = POOL.op(nc.gpsimd.tensor_scalar(dg[k][:], identb[:],
                                                              cs[:, C_CW + j * 31 + w:C_CW + j * 31 + w + 1], 1.0,
                                                              ALU.mult, ALU.mult))
                        PE.wait([t_d])
                        dg_free[k] = PE.op(nc.tensor.matmul(banks[b][:, 0:TT], dg[k][:], uext[:, j, w:w + TT],
                                                            start=(w == 0), stop=(w == 30)))
                    ACT.wait([dg_free[k]])
                    t_conv[j] = ACT.op(nc.scalar.activation(out=yc[:, j, :], in_=banks[b][:, 0:TT], func=AF.Identity,
                                                            bias=cs[:, C_CB + j:C_CB + j + 1], scale=1.0))
                    bank_free[b] = t_conv[j]
                c.t_conv = t_conv
                c.t_convmm = dg_free[k]

            t_q = []
            t_krot = [None, None]
            pend = []

            def rope2(cidx, zrb, zb, t_z, t_a):
                (b2,) = acquire(1)
                PE.wait([t_z])
                t_p = PE.op(nc.tensor.matmul(banks[b2][:, 0:tt], pmat[:], zb[:, 0:tt], start=True, stop=True))
                rope_zb_free[cidx % 2] = t_p
                DVE.wait([t_p, c.t_cs, state.get("ysq_rd1")])
                t_s2 = DVE.op(nc.vector.tensor_tensor(out=zq[:, 0:tt], in0=banks[b2][:, 0:tt], in1=cst_t[:, 1, 0:tt],
                                                      op=ALU.mult))
                bank_free[b2] = t_s2
                DVE.wait([t_s2, t_a])
                t_r = DVE.op(nc.vector.tensor_tensor(out=zrb[:, 0:tt], in0=zrb[:, 0:tt], in1=zq[:, 0:tt],
                                                     op=ALU.add))
                ACT.wait([t_r, a_ok, t_h])
                if cidx < 8:
                    t_w = ACT.op(nc.scalar.copy(out=c.qT[:, cidx, 0:tt], in_=zrb[:, 0:tt]))
                elif sample:
                    t_w = ACT.op(nc.scalar.copy(out=krot_s[:, cidx - 8, :], in_=zrb[:, 0:tt]))
                else:
                    ACT.op(nc.scalar.copy(out=kTa[0:64, cidx - 8, 128:128 + tt], in_=zrb[0:64, 0:tt]))
                    t_w = ACT.op(nc.scalar.copy(out=kTb[64:128, cidx - 8, 128:128 + tt], in_=zrb[64:128, 0:tt]))
                t_q.append(t_w)
                rope_zr_free[cidx % 2] = t_w
                if cidx >= 8 and last:
                    (b3,) = acquire(1)
                    PE.wait([t_r])
                    t_t = PE.op(nc.tensor.transpose(banks[b3][:, 0:128], zrb[:, 384:512], ident))
                    DVE.wait([t_t, state.get("kout_rd"), state.get("ln_rd")])
                    t_ko = DVE.op(nc.vector.tensor_copy(kout[:, (cidx - 8) * 128:(cidx - 7) * 128],
                                                        banks[b3][:, 0:128]))
                    bank_free[b3] = t_ko
                    t_krot[cidx - 8] = t_ko
                    rope_zr_free[cidx % 2] = t_ko

            for t in range(5):
                wi, wv = w_get()
                for j in range(2):
                    cidx = t * 2 + j
                    b, tm = fm_chunk(c, wv, j)
                    if j == 1:
                        w_done(wi, tm)
                    zb = zqb2[cidx % 2]
                    zrb = zr[cidx % 2]
                    ACT.wait([tm, rope_zb_free[cidx % 2]])
                    t_z = ACT.op(nc.scalar.activation(out=zb[:, 0:tt], in_=banks[b][:, 0:tt], func=AF.Identity,
                                                      bias=cs[:, C_BFM + 16 + cidx:C_BFM + 17 + cidx], scale=1.0))
                    DVE.wait([tm, t_z, c.t_cs, rope_zr_free[cidx % 2]])
                    t_a = DVE.op(nc.vector.scalar_tensor_tensor(
                        out=zrb[:, 0:tt], in0=banks[b][:, 0:tt], scalar=cs[:, C_BFM + 16 + cidx:C_BFM + 17 + cidx],
                        in1=cst_t[:, 0, 0:tt], op0=ALU.add, op1=ALU.mult))
                    bank_free[b] = t_a
                    pend.append((cidx, zrb, zb, t_z, t_a))
                    if len(pend) > 1:
                        rope2(*pend.pop(0))
            while pend:
                rope2(*pend.pop(0))
            if last:
                SP.wait(t_krot)
                state["kout_rd"] = dma(SP, sem_m, o_pk[c.seq, :, :], kout)
            c.t_q = t_q

        def ph_B2(c):
            tt, gs, ng, sample, last = c.tt, c.gs, c.ng, c.sample, c.last
            a_ok = state["abuf_rd"]
            t_h = c.t_h
            wi, wv = w_get()
            t_v = []
            for g in range(ng):
                (b,) = acquire(1)
                for kc in range(16):
                    ins = nc.tensor.matmul(banks[b][0:gs, 0:256], c.xnT[:, kc, g * gs:(g + 1) * gs], wv[:, kc, 0:256],
                                           start=(kc == 0), stop=(kc == 15))
                tm = PE.op(ins)
                vb = v32[g % 2]
                DVE.wait([tm, state.get("v32_rd%d" % (g % 2))])
                t1 = DVE.op(nc.vector.tensor_tensor(out=vb[0:gs, :], in0=banks[b][0:gs, 0:256], in1=bvb[0:gs, :],
                                                    op=ALU.add))
                bank_free[b] = t1
                if not sample:
                    ACT.wait([t1, a_ok, t_h])
                    t2 = ACT.op(nc.scalar.copy(out=Vt[:, 1 + g, :], in_=vb[:, :]))
                    t_v.append(t2)
                    state["v32_rd%d" % (g % 2)] = t2
                    if last and g == 3:
                        SP.wait([t1])
                        state["v32_rd1"] = dma(SP, sem_m, o_pv[c.seq, :, :], vb[:, :])
                else:
                    t_v.append(t1)
            w_done(wi, tm)
            tm_last_win[0] = tm
            c.t_v = t_v

        def ph_LN(c, split=False):
            tt, sample, last = c.tt, c.sample, c.last
            t_u = c.t_u
            if not sample:
                DVE.wait([c.t_convmm])
                c.t_uh = DVE.op(nc.vector.tensor_copy(uhalo[:], uext[:, :, TT:TT + 30]))
                if last:
                    (b3, b4) = acquire(2)
                    PE.wait(t_u)
                    for j in range(8):
                        bk = banks[b3] if j < 4 else banks[b4]
                        ins = nc.tensor.transpose(bk[0:30, (j % 4) * 128:(j % 4 + 1) * 128], ulast[:, j, :], ident)
                    t_t = PE.op(ins)
                    DVE.wait([t_t, state.get("ucout_rd")])
                    DVE.op(nc.vector.tensor_copy(ucout[0:30, 0:512], banks[b3][0:30, :]))
                    t_uc = DVE.op(nc.vector.tensor_copy(ucout[0:30, 512:1024], banks[b4][0:30, :]))
                    bank_free[b3] = t_uc
                    bank_free[b4] = t_uc
                    SP.wait([t_uc])
                    state["ucout_rd"] = dma(SP, sem_m, o_pc[c.seq, :, :], ucout[0:30, :])
                t_conv = c.t_conv
            else:
                t_conv = sample_conv(t_u)
                c.t_conv = t_conv
                c.t_uh = None

            c.t_conv_l = t_conv
            if not split:
                ln_stats(c)
                ln_mid(c)
                ln_rs(c)
                ln_chunks(c, 0, 8)

        def ln_stats(c):
            tt = c.tt
            t_conv = c.t_conv_l
            (bs1, bs2) = acquire(2)
            c.bs = (bs1, bs2)
            for j in range(8):
                yq = ysq[j % 2]
                ACT.wait([t_conv[j], state.get("ysq_rd%d" % (j % 2))])
                t_s = ACT.op(nc.scalar.activation(out=yq[:, 0:tt], in_=c.yc[:, j, 0:tt], func=AF.Square))
                PE.wait([t_conv[j], t_s])
                nc.tensor.matmul(banks[bs1][:, 0:tt], ones, c.yc[:, j, 0:tt], start=(j == 0), stop=(j == 7))
                state["ysq_rd%d" % (j % 2)] = PE.op(
                    nc.tensor.matmul(banks[bs2][:, 0:tt], ones, yq[:, 0:tt], start=(j == 0), stop=(j == 7)))
            c.t_st = state["ysq_rd1"]

        def ln_mid(c):
            tt = c.tt
            bs1, bs2 = c.bs
            t_st = c.t_st
            ACT.wait([t_st, state.get("ln_rd"), state.get("kout_rd")])
            t_mu = ACT.op(nc.scalar.activation(out=mu[:, 0:tt], in_=banks[bs1][:, 0:tt], func=AF.Copy, scale=1.0 / 1024))
            DVE.wait([t_mu, t_st, state.get("ln_rd")])
            t_1 = DVE.op(nc.vector.tensor_tensor(out=rs_ln[:, 0:tt], in0=mu[:, 0:tt], in1=mu[:, 0:tt], op=ALU.mult))
            DVE.wait([t_1])
            t_2 = DVE.op(nc.vector.scalar_tensor_tensor(out=rs_ln[:, 0:tt], in0=banks[bs2][:, 0:tt], scalar=1.0 / 1024,
                                                        in1=rs_ln[:, 0:tt], op0=ALU.mult, op1=ALU.subtract))
            bank_free[bs1] = t_2
            bank_free[bs2] = t_2
            c.t_2 = t_2

        def ln_rs(c):
            tt = c.tt
            ACT.wait([c.t_2])
            t_3 = ACT.op(nc.scalar.activation(out=rs_ln[:, 0:tt], in_=rs_ln[:, 0:tt], func=AF.Sqrt, scale=1.0,
                                              bias=cs[:, C_EPS:C_EPS + 1]))
            DVE.wait([t_3])
            c.t_4 = DVE.op(nc.vector.reciprocal(rs_ln[:, 0:tt], rs_ln[:, 0:tt]))
            c.t_co = []

        def ln_chunks(c, j0, j1):
            tt = c.tt
            t_conv = c.t_conv_l
            for j in range(j0, j1):
                DVE.wait([c.t_4, t_conv[j]])
                t_a = DVE.op(nc.vector.tensor_tensor(out=c.yc[:, j, 0:tt], in0=c.yc[:, j, 0:tt], in1=mu[:, 0:tt],
                                                     op=ALU.subtract))
                DVE.wait([t_a])
                t_b = DVE.op(nc.vector.tensor_tensor(out=c.yc[:, j, 0:tt], in0=c.yc[:, j, 0:tt], in1=rs_ln[:, 0:tt],
                                                     op=ALU.mult))
                ACT.wait([t_b, state["mix_rd"]])
                c.t_co.append(ACT.op(nc.scalar.activation(out=c.mixT[:, 8 + j, 0:tt], in_=c.yc[:, j, 0:tt],
                                                          func=AF.Silu, scale=cs[:, C_LG + j:C_LG + j + 1],
                                                          bias=cs[:, C_LB + j:C_LB + j + 1])))
                state["ln_rd"] = t_b
                state["yc_rd"] = c.t_co[-1]

        def ph_LNF(n, c):
            gs = c.gs

            def src_h(g, xb):
                ACT.wait([c.t_e, xb_free[g % 2], xb_st[g % 2]])
                return c.t_e, c.hres[0:gs, g, :]

            ph_LN(n, split=True)
            ln_stats(n)
            c.f3 = {}
            c.f3[0] = norm_pre(c, 0, src_h)
            c.f3[1] = norm_pre(c, 1, src_h)
            ln_mid(n)
            tF = norm_post(c, 0, c.f3[0], C_G2, c.xnT, tm_last_win[0])
            c.f3[2] = norm_pre(c, 2, src_h)
            ln_rs(n)
            ln_chunks(n, 0, 4)
            tF += norm_post(c, 1, c.f3[1], C_G2, c.xnT, tm_last_win[0])
            c.f3[3] = norm_pre(c, 3, src_h)
            ln_chunks(n, 4, 8)
            tF += norm_post(c, 2, c.f3[2], C_G2, c.xnT, tm_last_win[0])
            tF += norm_post(c, 3, c.f3[3], C_G2, c.xnT, tm_last_win[0])
            PE.wait(tF)

        def ph_F_pre01(c):
            gs = c.gs

            def src_h(g, xb):
                ACT.wait([c.t_e, xb_free[g % 2], xb_st[g % 2]])
                return c.t_e, c.hres[0:gs, g, :]

            c.f3 = {}
            c.f3[0] = norm_pre(c, 0, src_h)
            c.f3[1] = norm_pre(c, 1, src_h)

        def attn_gen(c):
            if c.sample:
                c.t_att = sample_attention(c.t_q, c.t_v)
                c.t_kh = None
                return
            first_seq_tile = c.first
            t_q, t_v = c.t_q, c.t_v
            items = [(g, cc_, b) for g in range(4) for cc_ in range(8) for b in range(2)]
            n = len(items)
            stA = {}
            outs = []
            a_free = state.setdefault("a_free", {})

            def s1(i):
                g, cc_, b = items[i]
                slot = cc_ * 2 + b
                col = i % 64
                u = i % NPB
                bS = 4 + (i % 2)
                PE.wait(t_q + t_v + [t_k8, a_free.get(("S", i % 2))])
                mi = 1 if (first_seq_tile and g == 0) else 0
                nc.tensor.matmul(banks[bS][:, 256:257], identb[:], skh[:, slot:slot + 1], start=True, stop=False)
                nc.tensor.matmul(banks[bS][:, 256:257], identb[:], skl[:, slot:slot + 1], start=False, stop=True)
                kTx = kTa if b == 0 else kTb
                nc.tensor.matmul(banks[bS][:, 0:256], qT[:, cc_, g * 128:(g + 1) * 128],
                                 kTx[:, cc_ // 4, g * 128:g * 128 + 256], start=True, stop=False)
                t_s = PE.op(nc.tensor.matmul(banks[bS][:, 0:256], identb[:], maskb[:, mi, :], start=False, stop=True))
                DVE.wait([t_s])
                t_m = DVE.op(nc.vector.reduce_max(sm_mx[:, col:col + 1], banks[bS][:, 0:257], AX.X))
                DVE.wait([t_m])
                t_nb = DVE.op(nc.vector.tensor_scalar(sm_nb[:, col:col + 1], sm_mx[:, col:col + 1], -0.125, None,
                                                      ALU.mult))
                ACT.wait([t_nb, a_free.get(("pb", u))])
                t_p = ACT.op(nc.scalar.activation(out=pb[u][:, 0:257], in_=banks[bS][:, 0:257], func=AF.Exp,
                                                  scale=0.125, bias=sm_nb[:, col:col + 1],
                                                  accum_out=sm_rs[:, col:col + 1]))
                a_free[("S", i % 2)] = t_p
                stA[("t_p", i)] = t_p

            def s2(i):
                col = i % 64
                u = i % NPB
                v = i % NPN
                DVE.wait([stA[("t_p", i)]])
                t_r = DVE.op(nc.vector.reciprocal(sm_rd[:, col:col + 1], sm_rs[:, col:col + 1]))
                ACT.wait([t_r, a_free.get(("pn", v))])
                t_n = ACT.op(nc.scalar.activation(out=pn[v][:], in_=pb[u][:, 0:256], func=AF.Copy,
                                                  scale=sm_rd[:, col:col + 1]))
                a_free[("pb", u)] = t_n
                stA[("t_n", i)] = t_n

            def s3a(i):
                v = i % NPN
                w = i % NPT
                qs = 0
                PE.wait([stA[("t_n", i)], a_free.get(("PT", qs))])
                ptv = banks[6][:].bitcast(BF16)[:, qs * 256:(qs + 1) * 256]
                nc.tensor.transpose(ptv[:, 0:128], pn[v][:, 0:128], identb[:])
                t_t = PE.op(nc.tensor.transpose(ptv[:, 128:256], pn[v][:, 128:256], identb[:]))
                a_free[("pn", v)] = t_t
                DVE.wait([t_t, a_free.get(("pts", w))])
                t_c2 = DVE.op(nc.vector.tensor_copy(pts[w][:], ptv))
                a_free[("PT", qs)] = t_c2
                stA[("t_c2", i)] = t_c2

            def s3b(i):
                g, cc_, b = items[i]
                w = i % NPT
                gkv = 2 * (cc_ // 4) + b
                ov = banks[7][:, 0:128]
                PE.wait([stA[("t_c2", i)], a_free.get(("O", 0)), a_free.get(("O", 1))])
                nc.tensor.matmul(ov[b * 64:(b + 1) * 64, :], Vt[:, g, gkv * 64:(gkv + 1) * 64],
                                 pts[w][:, 0:128], start=True, stop=False, skip_group_check=True)
                t_o = PE.op(nc.tensor.matmul(ov[b * 64:(b + 1) * 64, :], Vt[:, g + 1, gkv * 64:(gkv + 1) * 64],
                                             pts[w][:, 128:256], start=False, stop=True, skip_group_check=True))
                a_free[("pts", w)] = t_o
                ACT.wait([t_o, state["mix_rd"]])
                t_e = ACT.op(nc.scalar.copy(out=mixT_v[b * 64:(b + 1) * 64, cc_, g * 128:(g + 1) * 128],
                                            in_=ov[b * 64:(b + 1) * 64, :]))
                a_free[("O", b)] = t_e
                outs.append(t_e)

            K2, K3, K4 = 2, 4, 6
            for step in range(n + K4):
                if step < n:
                    s1(step)
                if 0 <= step - K2 < n:
                    s2(step - K2)
                if 0 <= step - K3 < n:
                    s3a(step - K3)
                if 0 <= step - K4 < n:
                    s3b(step - K4)
                yield
            DVE.wait(t_q + t_v)
            DVE.op(nc.vector.tensor_copy(khalo[0:64, :, :], kTa[0:64, :, TT:TT + 128]))
            DVE.op(nc.vector.tensor_copy(khalo[64:128, :, :], kTb[64:128, :, TT:TT + 128]))
            c.t_kh = DVE.op(nc.vector.tensor_copy(vhalo[:], Vt[:, 4, :]))
            c.t_att = outs

        def weave(gen, k):
            if gen is None:
                return
            for _ in range(k):
                try:
                    next(gen)
                except StopIteration:
                    return

        def drain(gen):
            if gen is None:
                return
            for _ in gen:
                pass

        def tm_gemm(c, src_T, gen, kweave, acc_fixed=None, acc_wait=()):
            ng, gs = c.ng, c.gs
            wi, wv = w_get()
            nb = 2 if ng == 4 else 1
            if acc_fixed is None:
                acc = acquire(nb)
            else:
                acc = acc_fixed
                PE.wait(list(acc_wait))
            for kc in range(16):
                for g in range(ng):
                    ins = nc.tensor.matmul(banks[acc[g // 2]][0:gs, (g % 2) * 256:(g % 2) * 256 + 256],
                                           src_T[:, kc, g * gs:(g + 1) * gs], wv[:, kc, 0:256],
                                           start=(kc == 0 and g % 2 == 0), stop=(kc == 15), skip_group_check=True)
            tm = PE.op(ins)
            w_done(wi, tm)
            if kweave:
                weave(gen, 1)
            return acc, tm

        def prep_x(c):
            gs = c.gs
            par = c.ti % 2
            POOL.wait(state.get("xbs_rd%d" % par) or [])
            c.t_prep = [None] * 4
            for g in range(c.ng):
                dst = xbs[par, g * gs:(g + 1) * gs, :]
                t1 = dma(POOL, sem_pb, dst, c.xsrc[c.r0 + g * gs:c.r0 + (g + 1) * gs, :])
                POOL.wait([t1])
                c.t_prep[g] = dma(POOL, sem_pb, dst, bo.partition_broadcast(128)[0:gs, :], accum=ALU.add)

        def ph_E(c, genA=None):
            ng, gs, sample = c.ng, c.gs, c.sample
            lst = state.get("hres_free") or [None] * 4
            t_x = [None] * 4
            par = c.ti % 2
            for g in range(ng):
                SP.wait(([lst[g]] if not sample else lst) + [c.t_prep[g]])
                t_x[g] = dma(SP, sem_hg[g], c.hres[0:gs, g, :], xbs[par, g * gs:(g + 1) * gs, :])
            c.t_xrl = t_x
            PE.wait(c.t_att + c.t_co)
            weave(genA, 1)
            for d in range(8):
                if d == 1:
                    weave(genA, 1)
                af = state.setdefault("a_free", {})
                if ng == 4 and d == 2:
                    acc, tm = tm_gemm(c, c.mixT, None, 0, acc_fixed=[4, 5],
                                      acc_wait=[af.get(("S", 0)), af.get(("S", 1))])
                elif ng == 4 and d == 3:
                    acc, tm = tm_gemm(c, c.mixT, None, 0, acc_fixed=[6, 7],
                                      acc_wait=[af.get(("PT", 0)), af.get(("O", 0)), af.get(("O", 1))])
                else:
                    acc, tm = tm_gemm(c, c.mixT, None, 0)
                for g in range(ng):
                    DVE.wait([tm, t_x[g]])
                    hv = c.hres[0:gs, g, d * 256:(d + 1) * 256]
                    t_e = DVE.op(nc.vector.tensor_tensor(
                        out=hv, in0=banks[acc[g // 2]][0:gs, (g % 2) * 256:(g % 2) * 256 + 256], in1=hv, op=ALU.add))
                    if acc[0] < 4:
                        bank_free[acc[g // 2]] = t_e
                if ng == 4 and d == 2:
                    af[("S", 0)] = t_e
                    af[("S", 1)] = t_e
                elif ng == 4 and d == 3:
                    af[("PT", 0)] = t_e
                    af[("O", 0)] = t_e
                    af[("O", 1)] = t_e
                if d in (1, 3, 5, 7):
                    weave(genA, 1)
            state["mix_rd"] = tm
            state["xbs_rd%d" % (c.ti % 2)] = [x_ for x_ in c.t_xrl if x_ is not None]
            c.t_e = t_e
            drain(genA)

        def ph_F(c):
            gs = c.gs

            def src_h(g, xb):
                ACT.wait([c.t_e, xb_free[g % 2], xb_st[g % 2]])
                return c.t_e, c.hres[0:gs, g, :]

            tF = norm_T(c, src_h, C_G2, c.xnT, tm_last_win[0])
            PE.wait(tF)

        def ph_G(c, gen):
            tt, ng, gs = c.tt, c.ng, c.gs
            ri = 0
            for q in range(4):
                t_h2 = []
                for t in range(8):
                    wi, wv = w_get()
                    for j in range(2):
                        b, tm = fm_chunk(c, wv, j, gen)
                        if j == 1:
                            w_done(wi, tm)
                        rix = ri % 2
                        rb = rl[rix]
                        ri += 1
                        ACT.wait([tm, state.get("rl_rd%d" % rix)])
                        t_r = ACT.op(nc.scalar.activation(out=rb[:, 0:tt], in_=banks[b][:, 0:tt], func=AF.Relu))
                        bank_free[b] = t_r
                        DVE.wait([t_r, state["ych_rd"], state.get("yc_rd")])
                        t_s = DVE.op(nc.vector.tensor_tensor(out=c.h2T[:, t * 2 + j, 0:tt], in0=rb[:, 0:tt],
                                                             in1=rb[:, 0:tt], op=ALU.mult))
                        state["rl_rd%d" % rix] = t_s
                        t_h2.append(t_s)
                        weave(gen, 1)
                tm_last_up = tm
                PE.wait(t_h2)
                for d in range(8):
                    acc, tm = tm_gemm(c, c.h2T, gen, 4)
                    DVE.wait([tm])
                    for g in range(ng):
                        hv = c.hres[0:gs, g, d * 256:(d + 1) * 256]
                        t_e = DVE.op(nc.vector.tensor_tensor(
                            out=hv, in0=banks[acc[g // 2]][0:gs, (g % 2) * 256:(g % 2) * 256 + 256], in1=hv,
                            op=ALU.add))
                        bank_free[acc[g // 2]] = t_e
                state["ych_rd"] = tm
            state["xn_rd"] = tm_last_up
            c.t_e = t_e
            drain(gen)

        def ph_H(c):
            gs = c.gs
            hfree = [None] * 4
            for g in range(c.ng):
                xb = xsb[g % 2]
                ACT.wait([c.t_e])
                ACT.op(nc.scalar.activation(out=sqj[0:gs, :], in_=c.hres[0:gs, g, :], func=AF.Square,
                                            accum_out=ss[0:gs, g:g + 1]))
                t1 = ACT.op(nc.scalar.activation(out=ss[0:gs, 8 + g:9 + g], in_=ss[0:gs, g:g + 1], func=AF.Sqrt,
                                                 scale=1.0 / D, bias=cs[0:gs, C_EPS:C_EPS + 1]))
                DVE.wait([t1])
                t2 = DVE.op(nc.vector.reciprocal(rstd[0:gs, g:g + 1], ss[0:gs, 8 + g:9 + g]))
                DVE.wait([t2, xb_free[g % 2], xb_st[g % 2]])
                t3 = DVE.op(nc.vector.scalar_tensor_tensor(out=xb[0:gs, :], in0=c.hres[0:gs, g, :],
                                                           scalar=rstd[0:gs, g:g + 1], in1=gfb[0:gs, :],
                                                           op0=ALU.mult, op1=ALU.mult))
                hfree[g] = t3
                SP.wait([t3])
                xb_st[g % 2] = dma(SP, sem_xo[g % 2], c.ydst[c.r0 + g * gs:c.r0 + (g + 1) * gs, :], xb[0:gs, :])
            state["hres_free"] = hfree

        def finish_attn(c):
            DVE.wait(c.t_att + [c.t_uh, c.t_kh] + list(c.t_conv))
            state["abuf_rd"] = DVE.op(nc.vector.memset(ss[:, 15:16], 0.0))

        def sample_conv(t_u):
            SP.wait([state["xn_rd"]] + (state.get("hres_free") or []))
            for a in range(4):
                t_l = dma(SP, sem_m, ccst[:, a, :], cc[4 * a:4 * a + 4, :, :].rearrange("s w c -> (s w) c"))
            PE.wait([t_l])
            tcp = []
            for a in range(4):
                for j in range(8):
                    (b,) = acquire(1)
                    t_t = PE.op(nc.tensor.transpose(banks[b][:, 0:120], ccst[:, a, j * 128:(j + 1) * 128],
                                                    ident[0:120, 0:120]))
                    DVE.wait([t_t, state["xn_rd"]])
                    t_c = DVE.op(nc.vector.tensor_copy(
                        ccT[:, j, 4 * a:4 * a + 4, :],
                        banks[b][:, 0:120].rearrange("p (s w) -> p s w", s=4)))
                    bank_free[b] = t_c
                    tcp.append(t_c)
            cwv = cs[:, C_CW:C_CW + 248].rearrange("p (j w) -> p j w", j=8)
            DVE.wait(tcp)
            t_2 = None
            for hf in range(2):
                DVE.wait([t_2])
                t_1 = DVE.op(nc.vector.tensor_tensor(
                    out=prod_h, in0=ccT[:, 4 * hf:4 * hf + 4, :, :],
                    in1=cwv[:, 4 * hf:4 * hf + 4, 0:30].unsqueeze(2).to_broadcast([128, 4, NS, 30]), op=ALU.mult))
                DVE.wait([t_1])
                t_2 = DVE.op(nc.vector.reduce_sum(yc_s[:, 4 * hf:4 * hf + 4, :], prod_h, AX.X))
            DVE.wait([t_2] + list(t_u))
            t_3 = DVE.op(nc.vector.tensor_tensor(out=prod_u, in0=uext_s[:, :, 30:30 + NS],
                                                 in1=cwv[:, :, 30:31].to_broadcast([128, 8, NS]), op=ALU.mult))
            DVE.wait([t_3])
            t_4 = DVE.op(nc.vector.tensor_tensor(out=yc_s[:], in0=yc_s[:], in1=prod_u, op=ALU.add))
            DVE.wait([t_4])
            t_5 = DVE.op(nc.vector.tensor_tensor(out=yc_s[:], in0=yc_s[:],
                                                 in1=cs[:, C_CB:C_CB + 8].unsqueeze(2).to_broadcast([128, 8, NS]),
                                                 op=ALU.add))
            (b,) = acquire(1)
            (b2,) = acquire(1)
            PE.wait(list(t_u))
            for j in range(8):
                bk = banks[b] if j < 4 else banks[b2]
                ins = nc.tensor.transpose(bk[0:NS, (j % 4) * 128:(j % 4 + 1) * 128], uext_s[:, j, 30:30 + NS], ident)
            t_t = PE.op(ins)
            DVE.wait([t_t])
            DVE.wait([state.get("ucout_rd")])
            DVE.op(nc.vector.tensor_copy(utm_s[:, 0:512], banks[b][0:NS, :]))
            t_c = DVE.op(nc.vector.tensor_copy(utm_s[:, 512:1024], banks[b2][0:NS, :]))
            bank_free[b] = t_c
            bank_free[b2] = t_c
            SP.wait([t_c])
            dma(SP, sem_o, o_sc[:, 29, :], utm_s)
            return [t_5] * 8

        def sample_attention(t_q, t_v):
            (b,) = acquire(1)
            PE.wait(t_q)
            for c2 in range(2):
                ins = nc.tensor.transpose(banks[b][0:NS, c2 * 128:(c2 + 1) * 128], krot_s[:, c2, :], ident)
            t_t = PE.op(ins)
            DVE.wait([t_t])
            t_k = DVE.op(nc.vector.tensor_copy(ktm_s[:], banks[b][0:NS, 0:256]))
            bank_free[b] = t_k
            SP.wait([t_k] + t_v)
            t_r1 = dma(SP, sem_sk, o_sk[:, 127, :], ktm_s[:])
            t_r2 = dma(SP, sem_sk, o_sv[:, 127, :], v32[0][0:NS, :])
            SP.wait([t_r1, t_r2, early_tok[0], state["ych_rd"], state["mix_rd"]])
            t_l = dma(SP, sem_m, Kc[:, :, :], o_sk.rearrange("s k d -> k s d"))
            t_l2 = dma(SP, sem_m, Vc[:, :, :], o_sv.rearrange("s k d -> k s d"))
            t_vb = None
            PE.wait([t_l2, t_l, xb_free_ref[1]])
            DVE.wait([xb_st[1]])
            tk = []
            for s in range(NS):
                (b,) = acquire(1)
                for c2 in range(2):
                    ins = nc.tensor.transpose(banks[b][:, c2 * 128:(c2 + 1) * 128], Kc[:, s, c2 * 128:(c2 + 1) * 128],
                                              ident)
                t_t = PE.op(ins)
                DVE.wait([t_t])
                t_c = DVE.op(nc.vector.tensor_copy(knT[:, :, s, :],
                                                   banks[b][:, 0:256].rearrange("p (c k) -> p c k", c=2)))
                bank_free[b] = t_c
                tk.append(t_c)
            (bS,) = acquire(1)
            PE.wait(tk)
            stv = banks[bS][:, 0:256].rearrange("p (c b s) -> p c b s", c=8, b=2)
            for s in range(NS):
                for gkv in range(4):
                    b_ = gkv % 2
                    c0 = 4 * (gkv // 2)
                    ins = nc.tensor.matmul(stv[:, c0:c0 + 4, b_, s], knT[b_ * 64:(b_ + 1) * 64, gkv // 2, s, :],
                                           qT_s[b_ * 64:(b_ + 1) * 64, c0:c0 + 4, s], start=True, stop=True)
            t_s = PE.op(ins)
            DVE.wait([t_s, state.get("ysq_rd1")])
            t_c = DVE.op(nc.vector.tensor_copy(st_sb[:], banks[bS][:, 0:256]))
            bank_free[bS] = t_c
            (b2,) = acquire(1)
            PE.wait([t_c])
            nc.tensor.transpose(banks[b2][:, 0:128], st_sb[:, 0:128], ident)
            t_t = PE.op(nc.tensor.transpose(banks[b2][:, 128:256], st_sb[:, 128:256], ident))
            sk_s = sm_mx[:, 0:2]
            SP.wait([t_k8])
            for hh in range(2):
                for sl in range(8):
                    slot = hh * 8 + sl
                    t_sk = dma(SP, sem_m, sk_s[sl * NS:(sl + 1) * NS, hh:hh + 1],
                               cst[0:NS, C_SK + slot:C_SK + slot + 1], slow=True)
            DVE.wait([t_t, t_sk])
            s2 = banks[b2][:, 0:256].rearrange("p (h k) -> p h k", h=2)
            DVE.op(nc.vector.reduce_max(sm_mx[:, 2:4], s2, AX.X))
            DVE.wait([(DVE.sem, DVE.sem.v)])
            DVE.op(nc.vector.tensor_scalar(sm_mx[:, 4:6], sk_s, 8.0, None, ALU.mult))
            DVE.wait([(DVE.sem, DVE.sem.v)])
            DVE.op(nc.vector.tensor_tensor(out=sm_mx[:, 2:4], in0=sm_mx[:, 2:4], in1=sm_mx[:, 4:6], op=ALU.max))
            DVE.wait([(DVE.sem, DVE.sem.v)])
            t_nb = DVE.op(nc.vector.tensor_scalar(sm_nb[:, 0:2], sm_mx[:, 2:4], -0.125, None, ALU.mult))
            ACT.wait([t_nb])
            for hh in range(2):
                ACT.op(nc.scalar.activation(out=pb[hh][:, 0:128], in_=s2[:, hh, :], func=AF.Exp, scale=0.125,
                                            bias=sm_nb[:, hh:hh + 1], accum_out=sm_rs[:, hh:hh + 1]))
                t_e = ACT.op(nc.scalar.activation(out=sm_es[:, hh:hh + 1], in_=sk_s[:, hh:hh + 1], func=AF.Exp,
                                                  scale=1.0, bias=sm_nb[:, hh:hh + 1]))
            bank_free[b2] = t_e
            DVE.wait([t_e])
            DVE.op(nc.vector.tensor_tensor(out=sm_rd[:, 0:2], in0=sm_rs[:, 0:2], in1=sm_es[:, 0:2], op=ALU.add))
            DVE.wait([(DVE.sem, DVE.sem.v)])
            t_r = DVE.op(nc.vector.reciprocal(sm_rd[:, 0:2], sm_rd[:, 0:2]))
            ACT.wait([t_r])
            for hh in range(2):
                t_n = ACT.op(nc.scalar.activation(out=pn_s[:, hh, :], in_=pb[hh][:, 0:128], func=AF.Copy,
                                                  scale=sm_rd[:, hh:hh + 1]))
            (b3,) = acquire(1)
            PE.wait([t_n])
            ptv = banks[b3]
            nc.tensor.transpose(ptv[:, 0:128], pn_s[:, 0, :], ident)
            t_t = PE.op(nc.tensor.transpose(ptv[:, 128:256], pn_s[:, 1, :], ident))
            ACT.wait([t_t])
            t_c = ACT.op(nc.scalar.copy(out=pt_s[:], in_=ptv[:, 0:256]))
            bank_free[b3] = t_c
            (bO,) = acquire(1)
            PE.wait([t_c, t_vb])
            ptv2 = pt_s[:].rearrange("p (c b s) -> p c b s", c=8, b=2)
            ov = banks[bO][:, 0:8 * NS].rearrange("p (c s) -> p c s", c=8)
            for s in range(NS):
                for gkv in range(4):
                    b_ = gkv % 2
                    c0 = 4 * (gkv // 2)
                    ins = nc.tensor.matmul(ov[b_ * 64:(b_ + 1) * 64, c0:c0 + 4, s],
                                           Vc[:, s, gkv * 64:(gkv + 1) * 64], ptv2[:, c0:c0 + 4, b_, s],
                                           start=True, stop=True)
            t_o = PE.op(ins)
            ACT.wait([t_o])
            t_e = ACT.op(nc.scalar.copy(out=mixT_s[:, 0:8, :], in_=ov))
            bank_free[bO] = t_e
            return [t_e]

        ctxs = [make_ctx(i) for i in range(NTILE + 1)]
        c0 = ctxs[0]
        prep_x(c0)
        ph_A(c0)
        t_gf = dma(SP, sem_c, gfb[:], gf.partition_broadcast(128))
        DVE.wait([t_gf])
        dma(SP, sem_o, o_sk[:, 0:127, :], ck[:, 1:128, :])
        dma(SP, sem_o, o_sv[:, 0:127, :], cv[:, 1:128, :])
        early_tok[0] = dma(SP, sem_o, o_sc[:, 0:29, :], cc[:, 1:30, :])
        ph_B(c0)
        ph_LN(c0)
        drain(attn_gen(c0))
        finish_attn(c0)
        prep_x(ctxs[1])
        ph_E(c0, ph_A_gen(ctxs[1]))
        for i in range(NTILE):
            c = ctxs[i]
            gen = None
            n_ = None
            if i + 1 < NTILE:
                n_ = ctxs[i + 1]
                ph_B1(n_)
                ph_B2(n_)
                ph_LNF(n_, c)
                gen = attn_gen(n_)
            else:
                ph_F(c)
            if i + 2 <= NTILE:
                prep_x(ctxs[i + 2])
            ph_G(c, gen)
            if n_ is not None:
                finish_attn(n_)
            ph_H(c)
            if n_ is not None:
                ph_E(n_, ph_A_gen(ctxs[i + 2]) if i + 2 < NTILE else None)
        s_ = ctxs[NTILE]
        ph_A(s_)
        ph_B(s_)
        ph_LN(s_)
        drain(attn_gen(s_))
        ph_E(s_)
        ph_F(s_)
        ph_G(s_, None)
        ph_H(s_)

        SP.wait([(sem_o, sem_o.v), (sem_m, sem_m.v), (sem_sk, sem_sk.v)] + [(x_, x_.v) for x_ in sem_xo])
    return nc


_NC_CACHE = {}


def _host_consts(inputs):
    f32 = np.float32
    cst = np.zeros((128, C_END), f32)
    cst[:, C_ID:C_ID + 128] = np.eye(128, dtype=f32)
    pm = np.zeros((128, 128), f32)
    for m in range(128):
        r = m % 64
        if r < 8:
            pm[m + 8, m] = 1.0
        elif r < 16:
            pm[m - 8, m] = 1.0
    cst[:, C_PM:C_PM + 128] = pm
    cst[:, C_ONE:C_ONE + 128] = 1.0
    i = np.arange(128)[:, None]
    j = np.arange(256)[None, :]
    mk = np.where(j < 128, j > i, (j - 128) <= i).astype(f32)
    cst[:, C_MK:C_MK + 256] = mk
    mk0 = mk.copy()
    mk0[:, 0:128] = 0.0
    cst[:, C_MK0:C_MK0 + 256] = mk0
    cst[:, C_G1:C_G1 + 16] = inputs["norm1_g"][0].reshape(16, 128).T
    cst[:, C_G2:C_G2 + 16] = inputs["norm2_g"][0].reshape(16, 128).T
    cst[:, C_CB:C_CB + 8] = inputs["conv_b"][0].reshape(8, 128).T
    cst[:, C_LG:C_LG + 8] = inputs["conv_ln_g"][0].reshape(8, 128).T
    cst[:, C_LB:C_LB + 8] = inputs["conv_ln_b"][0].reshape(8, 128).T
    cw = inputs["conv_w"][0]
    cst[:, C_CW:C_CW + 248] = cw.T.reshape(8, 128, 31).transpose(1, 0, 2).reshape(128, 248)
    return cst


def _col_perm():
    cols = []
    for jj in range(8):
        cols.append(np.arange(2560 + jj * 128, 2560 + (jj + 1) * 128))
        cols.append(np.arange(1536 + jj * 128, 1536 + (jj + 1) * 128))
    for c in range(8):
        for b in range(2):
            h = head_of(c, b)
            cols.append(np.arange(h * 64, (h + 1) * 64))
    cols.append(np.arange(1024, 1280))
    cols.append(np.arange(1280, 1536))
    return np.concatenate(cols)


def kernel(x_prompt, x_sample, cache_k, cache_v, cache_conv, norm1_g, w_in, b_in, attn_sinks,
           conv_w, conv_b, conv_ln_g, conv_ln_b, w_out, b_out, norm2_g, w_up, w_down, final_norm_g):
    f32 = np.float32
    inputs = dict(norm1_g=norm1_g, norm2_g=norm2_g, conv_b=conv_b, conv_ln_g=conv_ln_g, conv_ln_b=conv_ln_b,
                  conv_w=conv_w)
    inputs = {k: np.asarray(v, f32) for k, v in inputs.items()}
    cst = _host_consts(inputs)
    perm = _col_perm()
    w_in_p = np.ascontiguousarray(np.asarray(w_in, f32)[0][:, perm])
    b_in_p = np.asarray(b_in, f32)[0][perm]
    cst[:, C_BFM:C_BFM + 26] = b_in_p[0:26 * 128].reshape(26, 128).T
    sinks = np.asarray(attn_sinks, f32)[0]
    sk = np.array([sinks[head_of(c, b)] for c in range(8) for b in range(2)], f32)
    cst[:, C_SK:C_SK + 16] = sk[None, :]
    cst[:, C_EPS] = EPS
    bvv = np.ascontiguousarray(b_in_p[26 * 128:26 * 128 + 256])
    rows = []
    for c in range(8):
        for b in range(2):
            h = head_of(c, b)
            rows.append(np.arange(h * 64, (h + 1) * 64))
    rows.append(np.arange(1024, 2048))
    rows = np.concatenate(rows)
    w_out_p = np.ascontiguousarray(np.asarray(w_out, f32)[0][rows, :])
    w_up_ = np.asarray(w_up, f32)[0]
    w_down_ = np.asarray(w_down, f32)[0]

    def tile_of(W, r0, c0):
        return W[r0:r0 + 2048, c0:c0 + 256].reshape(16, 128, 256).transpose(1, 0, 2).reshape(128, 4096)

    wt = np.empty((86, 128, 4096), f32)
    ti_ = 0
    for t in range(14):
        wt[ti_] = tile_of(w_in_p, 0, t * 256); ti_ += 1
    for d in range(8):
        wt[ti_] = tile_of(w_out_p, 0, d * 256); ti_ += 1
    for q in range(4):
        for t in range(8):
            wt[ti_] = tile_of(w_up_, 0, q * 2048 + t * 256); ti_ += 1
        for d in range(8):
            wt[ti_] = tile_of(w_down_, q * 2048, d * 256); ti_ += 1
    half = 8
    inv_freq = np.power(f32(500000.0), -np.arange(half, dtype=f32) * f32(2.0) / f32(16)).astype(f32)

    def tables(pos):
        ang = pos.astype(f32)[:, None] * inv_freq[None, :]
        co = np.cos(ang).astype(f32).T
        si = np.sin(ang).astype(f32).T
        T = pos.shape[0]
        tab = np.zeros((128, 2, T), f32)
        tab[:, 0, :] = 1.0
        for base in (0, 64):
            tab[base:base + 8, 0, :] = co
            tab[base + 8:base + 16, 0, :] = co
            tab[base:base + 8, 1, :] = -si
            tab[base + 8:base + 16, 1, :] = si
        return tab

    csp = tables(np.arange(SEQ, dtype=np.int32))
    css = tables(np.full((NS,), 16384, dtype=np.int32))

    if "nc" not in _NC_CACHE:
        _NC_CACHE["nc"] = build_nc()
    nc = _NC_CACHE["nc"]

    xpa = np.asarray(x_prompt, f32)
    xsa = np.asarray(x_sample, f32)
    cka = np.asarray(cache_k, f32)[0].reshape(128, 128, 256)
    cva = np.asarray(cache_v, f32)[0].reshape(128, 128, 256)
    cca = np.asarray(cache_conv, f32)[0]
    gfv = np.ascontiguousarray(np.asarray(final_norm_g, f32))
    bov = np.ascontiguousarray(np.asarray(b_out, f32)[0])
    in_maps = []
    for c in range(NCORE):
        in_maps.append({
            "xp": np.ascontiguousarray(xpa[2 * c:2 * c + 2].reshape(TOK, D)),
            "xs": np.ascontiguousarray(xsa[NS * c:NS * (c + 1), 0, :]),
            "ck": np.ascontiguousarray(cka[NS * c:NS * (c + 1)]),
            "cv": np.ascontiguousarray(cva[NS * c:NS * (c + 1)]),
            "cc": np.ascontiguousarray(cca[NS * c:NS * (c + 1)]),
            "wt": wt,
            "cst": cst, "csp": csp, "css": css, "gf": gfv, "bo": bov, "bv": bvv,
        })
    res = run_bass_kernel_spmd(nc, in_maps, core_ids=list(range(NCORE)))
    rs = res.results
    y_p = np.concatenate([r["y_p"].reshape(2, SEQ, D) for r in rs], axis=0)
    y_s = np.concatenate([r["y_s"].reshape(NS, 1, D) for r in rs], axis=0)
    pk = np.concatenate([r["o_pk"].reshape(2, 128, 4, 64) for r in rs], axis=0)[None]
    pv = np.concatenate([r["o_pv"].reshape(2, 128, 4, 64) for r in rs], axis=0)[None]
    pc = np.concatenate([r["o_pc"].reshape(2, 30, 1024) for r in rs], axis=0)[None]
    sk_ = np.concatenate([r["o_sk"].reshape(NS, 128, 4, 64) for r in rs], axis=0)[None]
    sv_ = np.concatenate([r["o_sv"].reshape(NS, 128, 4, 64) for r in rs], axis=0)[None]
    sc_ = np.concatenate([r["o_sc"].reshape(NS, 30, 1024) for r in rs], axis=0)[None]
    return (y_p, y_s, pk, pv, pc, sk_, sv_, sc_)
```

```python
import numpy as np
from contextlib import ExitStack
import concourse.bass as bass
import concourse.mybir as mybir
from concourse.bass_utils import run_bass_kernel_spmd

F32 = mybir.dt.float32
BF16 = mybir.dt.bfloat16
AF = mybir.ActivationFunctionType
ALU = mybir.AluOpType
AX = mybir.AxisListType

D = 2048
NCORE = 8
TOK = 4096
TT = 512
NTILE = TOK // TT
SEQ = 2048
NS = 16
DFF = 8192
EPS = 1e-5
NSLOT = 3

C_ID = 0
C_PM = 128
C_ONE = 256
C_MK = 384
C_MK0 = 640
C_G1 = 896
C_G2 = 912
C_CB = 928
C_LG = 936
C_LB = 944
C_CW = 952
C_BFM = 1200
C_SK = 1232
C_EPS = 1248
C_END = 1249


def head_of(c, b):
    g = 2 * (c // 4) + b
    return 4 * g + (c % 4)


class Sem:
    def __init__(self, h):
        self.h = h
        self.v = 0


class Eng:
    def __init__(self, h, semh):
        self.h = h
        self.sem = Sem(semh)
        self.waited = {}

    def wait(self, toks):
        for t in toks:
            if t is None:
                continue
            sm, v = t
            if self.waited.get(id(sm), 0) >= v:
                continue
            self.h.wait_ge(sm.h, v)
            self.waited[id(sm)] = v

    def op(self, ins):
        self.sem.v += 1
        ins.then_inc(self.sem.h, 1)
        return (self.sem, self.sem.v)


def build_nc():
    nc = bass.Bass("TRN2", target_bir_lowering=False)

    def din(name, shape):
        return nc.dram_tensor(name, list(shape), F32, kind="ExternalInput").ap()

    def dout(name, shape):
        return nc.dram_tensor(name, list(shape), F32, kind="ExternalOutput").ap()

    xp = din("xp", [TOK, D])
    xs_d = din("xs", [NS, D])
    ck = din("ck", [NS, 128, 256])
    cv = din("cv", [NS, 128, 256])
    cc = din("cc", [NS, 30, 1024])
    wt = din("wt", [86, 128, 4096])
    cst = din("cst", [128, C_END])
    csp = din("csp", [128, 2, SEQ])
    css = din("css", [128, 2, NS])
    gf = din("gf", [D])
    bo = din("bo", [D])
    bv = din("bv", [256])

    wsc = nc.dram_tensor("wsc", [86, 128, 4096], BF16, kind="Internal").ap()
    xbs = nc.dram_tensor("xbs", [2, TT, D], F32, kind="Internal").ap()

    y_p = dout("y_p", [TOK, D])
    y_s = dout("y_s", [NS, D])
    o_pk = dout("o_pk", [2, 128, 256])
    o_pv = dout("o_pv", [2, 128, 256])
    o_pc = dout("o_pc", [2, 30, 1024])
    o_sk = dout("o_sk", [NS, 128, 256])
    o_sv = dout("o_sv", [NS, 128, 256])
    o_sc = dout("o_sc", [NS, 30, 1024])

    with ExitStack() as es:
        def sb(name, shape, dt=F32):
            return es.enter_context(nc.sbuf_tensor(name, list(shape), dt))

        def newsem(name):
            return es.enter_context(nc.semaphore(name))

        cs = sb("cs", [128, C_END])
        xsb = [sb("xsb0", [128, D]), sb("xsb1", [128, D])]
        sqj = sb("sqj", [128, D], BF16)
        xnT = sb("xnT", [128, 16 * TT], BF16)
        hres_t = sb("hres_t", [128, 8192])
        abuf = sb("abuf", [128, 12272], BF16)
        ych = sb("ych", [128, 4096])
        mixT = sb("mixT", [128, 16 * TT], BF16)
        wr = [sb("wr%d" % i, [128, 4096], BF16) for i in range(NSLOT)]
        cst_t = sb("cst_t", [128, 2, TT])
        gfb = sb("gfb", [128, D])
        bvb = sb("bvb", [128, 256])
        zq = sb("zq", [128, TT])
        zqb2 = [sb("zqb0", [128, TT], BF16), sb("zqb1", [128, TT], BF16)]
        zr = [sb("zr0", [128, TT]), sb("zr1", [128, TT])]
        sgt = sb("sgt", [128, 2 * TT])
        sg = [sgt[:, 0:TT], sgt[:, TT:2 * TT]]
        ucout = sgt[0:32, :]
        v32 = [sb("v320", [128, 256]), sb("v321", [128, 256])]
        NPB = 3
        pb = [sb("pb%d" % i, [128, 260]) for i in range(NPB)]
        NPN = 5
        pn = [sb("pn%d" % i, [128, 256], BF16) for i in range(NPN)]
        NPT = 3
        pts = [sb("pts%d" % i, [128, 256], BF16) for i in range(NPT)]
        dg = [sb("dg%d" % i, [128, 128], BF16) for i in range(6)]
        maskb = sb("maskb", [128, 2, 256], BF16)
        ulast = sb("ulast", [128, 8, 30])
        sm_mx = sb("sm_mx", [128, 64])
        sm_nb = sb("sm_nb", [128, 64])
        sm_es = sb("sm_es", [128, 64])
        sm_rs = sb("sm_rs", [128, 64])
        sm_rd = sb("sm_rd", [128, 64])
        sk8 = sb("sk8", [128, 16])
        skh = sb("skh", [128, 16], BF16)
        skl = sb("skl", [128, 16], BF16)
        ysq1 = sb("ysq0", [128, TT])
        ysq = [ysq1, None]
        ysq[1] = zq
        mu = sb("mu", [128, TT])
        kout = mu[:, 0:256]
        rs_ln = sb("rs_ln", [128, TT])
        rl = [sb("rl0", [128, TT]), sb("rl1", [128, TT])]
        ss = sb("ss", [128, 16])
        rstd = sb("rstd", [128, 16])
        khalo = sb("khalo", [128, 2, 128], BF16)
        vhalo = sb("vhalo", [128, 256], BF16)
        uhalo = sb("uhalo", [128, 8, 30], BF16)
        xnT_s = sb("xnT_s", [128, 16, NS], BF16)
        mixT_s = sb("mixT_s", [128, 16, NS], BF16)
        h2T_s = sb("h2T_s", [128, 16, NS], BF16)
        yc_s = sb("yc_s", [128, 8, NS])
        qT_s = sb("qT_s", [128, 8, NS], BF16)
        krot_s = sb("krot_s", [128, 2, NS])
        uext_s = sb("uext_s", [128, 8, 30 + NS])
        st_sb = zq[:, 0:256]
        pn_s = zr[0][:, 0:256].rearrange("p (h k) -> p h k", h=2)
        pt_s = zr[1][:, 0:256]
        ktm_s = rl[0][0:NS, 0:256]

        R = hres_t
        xnT_v = xnT[:].rearrange("p (k t) -> p k t", k=16)
        mixT_v = mixT[:].rearrange("p (k t) -> p k t", k=16)
        hres = hres_t[:].rearrange("p (g d) -> p g d", g=4)
        qT = abuf[:, 0:4096].rearrange("p (k t) -> p k t", k=8)
        kTa = abuf[:, 4096:4096 + 1280].rearrange("p (k t) -> p k t", k=2)
        kTb = abuf[:, 10992:10992 + 1280].rearrange("p (k t) -> p k t", k=2)
        Vt = abuf[:, 5376:5376 + 1280].rearrange("p (g d) -> p g d", g=5)
        uext = abuf[:, 6656:6656 + 8 * 542].rearrange("p (j t) -> p j t", j=8)
        yc = ych[:].rearrange("p (j t) -> p j t", j=8)
        h2T = ych[:].bitcast(BF16).rearrange("p (k t) -> p k t", k=16)
        Kc = ych[:].rearrange("p (s d) -> p s d", s=NS)
        Vc = mixT[:].bitcast(F32).rearrange("p (s d) -> p s d", s=NS)
        ccT = xnT[:].bitcast(F32)[:, 0:8 * NS * 30].rearrange("p (j s w) -> p j s w", j=8, s=NS)
        ccst = R[0:120, 2048:2048 + 4096].rearrange("p (a c) -> p a c", a=4)
        hres_s = R[0:NS, 0:2048]
        knT = xsb[1][:].bitcast(BF16).rearrange("p (c s k) -> p c s k", c=2, s=NS)
        utm_s = ucout[0:NS, :]
        prod_h = R[:, 6144:6144 + 4 * NS * 30].rearrange("p (j s w) -> p j s w", j=4, s=NS)
        prod_u = R[:, 6144:6144 + 8 * NS].rearrange("p (j s) -> p j s", j=8)

        ident = cs[:, C_ID:C_ID + 128]
        pmat = sb("pmat", [128, 128], BF16)
        ones = cs[:, C_ONE:C_ONE + 128]
        identb = sb("identb", [128, 128], BF16)

        banks = [es.enter_context(nc.psum_tensor("bk%d" % i, [128, 512], F32)) for i in range(8)]
        bank_free = [None] * 8
        bank_ptr = [0]

        PE = Eng(nc.tensor, newsem("s_pe"))
        ACT = Eng(nc.scalar, newsem("s_act"))
        DVE = Eng(nc.vector, newsem("s_dve"))
        POOL = Eng(nc.gpsimd, newsem("s_pool"))
        SP = Eng(nc.sync, newsem("s_sp"))
        sem_w = [Sem(newsem("s_w%d" % i)) for i in range(NSLOT)]
        sem_x = [Sem(newsem("s_x0")), Sem(newsem("s_x1"))]
        sem_c = Sem(newsem("s_c"))
        sem_hg = [Sem(newsem("s_h%d" % i)) for i in range(4)]
        sem_bg = [Sem(newsem("s_b%d" % i)) for i in range(4)]
        sem_pb = Sem(newsem("s_pb"))
        sem_sk = Sem(newsem("s_sk"))
        sem_xo = [Sem(newsem("s_xo%d" % i)) for i in range(2)]
        sem_o = Sem(newsem("s_o"))
        sem_t = Sem(newsem("s_t"))
        sem_m = Sem(newsem("s_m"))
        sem_ws = [Sem(newsem("s_ws%d" % i)) for i in range(NSLOT)]

        def dma(eng, sem, out, in_, slow=False, accum=None):
            kw = {}
            if slow:
                kw["allow_slow_non_contiguous"] = True
            if accum is not None:
                kw["accum_op"] = accum
            eng.h.dma_start(out=out, in_=in_, **kw).then_inc(sem.h, 16)
            sem.v += 16
            return (sem, sem.v)

        def acquire(n=1):
            if bank_ptr[0] + n > 4:
                bank_ptr[0] = 0
            idx = list(range(bank_ptr[0], bank_ptr[0] + n))
            bank_ptr[0] = (bank_ptr[0] + n) % 4
            PE.wait([bank_free[i] for i in idx])
            return idx

        one_pass = [wt[t, :, :] for t in range(86)]
        NPW = len(one_pass)
        ID_B = list(range(0, 14))
        ID_E = list(range(14, 22))
        ID_G = list(range(22, 86))
        order = ID_B + ID_E
        for i in range(NTILE):
            if i + 1 < NTILE:
                order += ID_B
            order += ID_G
            if i + 1 < NTILE:
                order += ID_E
        order += ID_B + ID_E + ID_G
        seen = set()
        first_use = []
        for t in order:
            first_use.append(t not in seen)
            seen.add(t)
        w_issued = [0]
        w_load_tok = {}
        w_free_tok = {}
        w_next = [0]
        w_sc_tok = {}
        w_slot_sc = {}

        def w_prefetch(upto):
            while w_issued[0] <= upto and w_issued[0] < len(order):
                i = w_issued[0]
                tid = order[i]
                s = i % NSLOT
                prev = i - NSLOT
                if first_use[i]:
                    if prev >= 0:
                        POOL.wait([w_free_tok[prev], w_slot_sc.get(prev)])
                    src = one_pass[tid]
                    dma(POOL, sem_w[s], wr[s][:, 0:2048], src[:, 0:2048])
                    w_load_tok[i] = dma(POOL, sem_w[s], wr[s][:, 2048:4096], src[:, 2048:4096])
                else:
                    if prev >= 0:
                        SP.wait([w_free_tok[prev], w_slot_sc.get(prev)])
                    SP.wait([w_sc_tok[tid]])
                    w_load_tok[i] = dma(SP, sem_w[s], wr[s][:], wsc[tid, :, :])
                w_issued[0] += 1

        def w_get():
            i = w_next[0]
            w_next[0] += 1
            w_prefetch(i + NSLOT - 1)
            PE.wait([w_load_tok[i]])
            if first_use[i]:
                SP.wait([w_load_tok[i]])
                w_sc_tok[order[i]] = dma(SP, sem_ws[i % NSLOT], wsc[order[i], :, :], wr[i % NSLOT][:])
                w_slot_sc[i] = w_sc_tok[order[i]]
            return i, wr[i % NSLOT][:].rearrange("p (a b) -> p a b", a=16)

        def w_done(i, tok):
            w_free_tok[i] = tok

        t_c = dma(SP, sem_c, cs[:], cst[:, :])
        t_c = dma(SP, sem_c, bvb[:], bv.partition_broadcast(128))
        DVE.wait([t_c])
        DVE.op(nc.vector.tensor_copy(pmat[:], cs[:, C_PM:C_PM + 128]))
        DVE.op(nc.vector.tensor_copy(identb[:], ident))
        t_k8 = DVE.op(nc.vector.tensor_scalar(sk8[:], cs[:, C_SK:C_SK + 16], 8.0, None, ALU.mult))
        DVE.wait([t_k8])
        t_k8 = DVE.op(nc.vector.tensor_copy(skh[:], sk8[:]))
        DVE.wait([t_k8])
        t_k8 = DVE.op(nc.vector.tensor_tensor(out=skl[:], in0=sk8[:], in1=skh[:], op=ALU.subtract))
        DVE.op(nc.vector.tensor_scalar(maskb[:, 0, :], cs[:, C_MK:C_MK + 256], -1.0, 30000.0, ALU.add, ALU.mult))
        DVE.op(nc.vector.tensor_scalar(maskb[:, 1, :], cs[:, C_MK0:C_MK0 + 256], -1.0, 30000.0, ALU.add, ALU.mult))
        DVE.op(nc.vector.memset(kTa[:], 0.0))
        DVE.op(nc.vector.memset(kTb[:], 0.0))
        DVE.op(nc.vector.memset(khalo[:], 0.0))
        DVE.op(nc.vector.memset(vhalo[:], 0.0))
        t_init = DVE.op(nc.vector.memset(uhalo[:], 0.0))
        for e in (PE, ACT, POOL):
            e.wait([t_c, t_init])

        early_tok = [None]
        state = {"hres_rd": None, "xn_rd": None, "ych_rd": None, "mix_rd": None, "abuf_rd": None}
        tm_last_win = [None]
        dg_ctr = [0]
        dg_free = [None] * 6
        xb_free_ref = [None, None]
        xb_free = xb_free_ref
        rope_zb_free = [None, None]
        rope_zr_free = [None, None]
        xb_st = [None, None]

        class Ctx:
            pass

        def make_ctx(ti):
            c = Ctx()
            c.ti = ti
            c.sample = ti == NTILE
            if c.sample:
                c.tt, c.gs, c.ng = NS, NS, 1
                c.xnT, c.mixT, c.h2T, c.yc, c.qT = xnT_s[:], mixT_s[:], h2T_s[:], yc_s[:], qT_s[:]
                c.uext = uext_s[:]
                c.hres = hres_s.rearrange("p (g d) -> p g d", g=1)
                c.first = False
                c.last = False
                c.seq = 0
                c.pos0 = 0
                c.xsrc = xs_d
                c.ydst = y_s
                c.r0 = 0
            else:
                c.tt, c.gs, c.ng = TT, 128, 4
                c.xnT, c.mixT, c.h2T, c.yc, c.qT = xnT_v, mixT_v, h2T, yc, qT
                c.uext = uext
                c.hres = hres
                c.first = (ti % 4) == 0
                c.last = (ti % 4) == 3
                c.seq = ti // 4
                c.pos0 = (ti % 4) * TT
                c.xsrc = xp
                c.ydst = y_p
                c.r0 = ti * TT
            return c

        def norm_pre(c, g, src_fn):
            gs = c.gs
            xb = xsb[g % 2]
            t_src, src = src_fn(g, xb)
            ACT.wait([t_src])
            ACT.op(nc.scalar.activation(out=sqj[0:gs, :], in_=src, func=AF.Square,
                                        accum_out=ss[0:gs, g:g + 1]))
            t1 = ACT.op(nc.scalar.activation(out=ss[0:gs, 8 + g:9 + g], in_=ss[0:gs, g:g + 1], func=AF.Sqrt,
                                             scale=1.0 / D, bias=cs[0:gs, C_EPS:C_EPS + 1]))
            DVE.wait([t1])
            t2 = DVE.op(nc.vector.reciprocal(rstd[0:gs, g:g + 1], ss[0:gs, 8 + g:9 + g]))
            ACT.wait([t2])
            return ACT.op(nc.scalar.activation(out=xb[0:gs, :], in_=src, func=AF.Copy,
                                               scale=rstd[0:gs, g:g + 1]))

        def norm_post(c, g, t3, gcol, dstT, extra_wait):
            gs = c.gs
            xb = xsb[g % 2]
            toks = []
            PE.wait([t3])
            for k0 in range(0, 16, 4):
                (b,) = acquire(1)
                bv_ = banks[b][:].rearrange("p (a t) -> p a t", a=4)
                for a in range(4):
                    kc = k0 + a
                    ins = nc.tensor.transpose(bv_[:, a, 0:gs], xb[0:gs, kc * 128:(kc + 1) * 128],
                                              ident[0:gs, 0:gs])
                t4 = PE.op(ins)
                DVE.wait([t4, extra_wait])
                t5 = DVE.op(nc.vector.tensor_tensor(
                    out=dstT[:, k0:k0 + 4, g * gs:(g + 1) * gs], in0=bv_[:, :, 0:gs],
                    in1=cs[:, gcol + k0:gcol + k0 + 4].unsqueeze(2).to_broadcast([128, 4, gs]),
                    op=ALU.mult))
                bank_free[b] = t5
                toks.append(t5)
            xb_free[g % 2] = t4
            return toks

        def norm_T(c, src_fn, gcol, dstT, extra_wait):
            toks = []
            for g in range(c.ng):
                t3 = norm_pre(c, g, src_fn)
                toks += norm_post(c, g, t3, gcol, dstT, extra_wait)
            return toks

        def fm_chunk(c, wv, j, gen=None):
            (b,) = acquire(1)
            for kc in range(16):
                ins = nc.tensor.matmul(banks[b][:, 0:c.tt], wv[:, kc, j * 128:(j + 1) * 128], c.xnT[:, kc, 0:c.tt],
                                       start=(kc == 0), stop=(kc == 15))
            return b, PE.op(ins)

        def ph_A_gen(c):
            gs = c.gs
            SP.wait([state["abuf_rd"]])
            if c.sample:
                c.t_cs = dma(SP, sem_t, cst_t[:, :, 0:c.tt], css[:, :, :])
            else:
                c.t_cs = dma(SP, sem_t, cst_t[:, :, :], csp[:, :, c.pos0:c.pos0 + TT])

            def src_x(g, xb):
                SP.wait([xb_free[g % 2], xb_st[g % 2]])
                return dma(SP, sem_x[g % 2], xb[0:gs, :], c.xsrc[c.r0 + g * gs:c.r0 + (g + 1) * gs, :]), xb[0:gs, :]

            if not c.sample:
                DVE.wait([state["abuf_rd"]])
                if c.first:
                    DVE.op(nc.vector.memset(kTa[0:64, :, 0:128], 0.0))
                    DVE.op(nc.vector.memset(kTb[64:128, :, 0:128], 0.0))
                    DVE.op(nc.vector.memset(Vt[:, 0, :], 0.0))
                    c.t_h = DVE.op(nc.vector.memset(uext[:, :, 0:30], 0.0))
                else:
                    DVE.op(nc.vector.tensor_copy(kTa[0:64, :, 0:128], khalo[0:64, :, :]))
                    DVE.op(nc.vector.tensor_copy(kTb[64:128, :, 0:128], khalo[64:128, :, :]))
                    DVE.op(nc.vector.tensor_copy(Vt[:, 0, :], vhalo[:]))
                    c.t_h = DVE.op(nc.vector.tensor_copy(uext[:, :, 0:30], uhalo[:]))
            else:
                c.t_h = None
            tA = []
            t3 = {}
            ng = c.ng
            order_ = []
            if ng == 1:
                order_ = [("pre", 0), ("y",), ("post", 0)]
            else:
                order_ = [("pre", 0), ("y",), ("pre", 1), ("y",), ("post", 0), ("pre", 2), ("y",),
                          ("post", 1), ("pre", 3), ("y",), ("post", 2), ("y",), ("post", 3)]
            for it in order_:
                if it[0] == "pre":
                    t3[it[1]] = norm_pre(c, it[1], src_x)
                elif it[0] == "post":
                    tA += norm_post(c, it[1], t3[it[1]], C_G1, c.xnT, state["xn_rd"])
                else:
                    yield
            c.tA = tA

        def ph_A(c):
            drain(ph_A_gen(c))

        def ph_B(c):
            ph_B1(c)
            ph_B2(c)

        def ph_B1(c):
            tt, gs, ng, sample, last = c.tt, c.gs, c.ng, c.sample, c.last
            a_ok = state["abuf_rd"]
            t_h = c.t_h
            PE.wait(c.tA)
            t_u = [None] * 8
            for j in range(8):
                wi, wv = w_get()
                b, tm = fm_chunk(c, wv, 0)
                ACT.wait([tm, state.get("ucout_rd")])
                t_sg = ACT.op(nc.scalar.activation(out=sg[j % 2][:, 0:tt], in_=banks[b][:, 0:tt], func=AF.Sigmoid,
                                                   bias=cs[:, C_BFM + 2 * j:C_BFM + 2 * j + 1], scale=1.0))
                bank_free[b] = t_sg
                b, tm = fm_chunk(c, wv, 1)
                w_done(wi, tm)
                DVE.wait([tm, t_sg, a_ok, t_h])
                t_u[j] = DVE.op(nc.vector.scalar_tensor_tensor(
                    out=c.uext[:, j, 30:30 + tt], in0=banks[b][:, 0:tt],
                    scalar=cs[:, C_BFM + 2 * j + 1:C_BFM + 2 * j + 2], in1=sg[j % 2][:, 0:tt],
                    op0=ALU.add, op1=ALU.mult))
                if last:
                    DVE.wait([state.get("ucout_rd")])
                    t_u[j] = DVE.op(nc.vector.scalar_tensor_tensor(
                        out=ulast[:, j, :], in0=banks[b][:, TT - 30:TT],
                        scalar=cs[:, C_BFM + 2 * j + 1:C_BFM + 2 * j + 2], in1=sg[j % 2][:, TT - 30:TT],
                        op0=ALU.add, op1=ALU.mult))
                bank_free[b] = t_u[j]
                ACT.wait([t_u[j]])
            c.t_u = t_u

            c.t_conv = None
            if not sample:
                t_conv = [None] * 8
                ACT.wait([state["ych_rd"]])
                for j in range(8):
                    (b,) = acquire(1)
                    PE.wait([t_u[j]])
                    for w in range(31):
                        k = dg_ctr[0] % 6
                        dg_ctr[0] += 1
                        POOL.wait([dg_free[k]])
                        t_d = POOL.op(nc.gpsimd.tensor_scalar(dg[k][:], identb[:],
                                                              cs[:, C_CW + j * 31 + w:C_CW + j * 31 + w + 1], 1.0,
                                                              ALU.mult, ALU.mult))
                        PE.wait([t_d])
                        dg_free[k] = PE.op(nc.tensor.matmul(banks[b][:, 0:TT], dg[k][:], uext[:, j, w:w + TT],
                                                            start=(w == 0), stop=(w == 30)))
                    ACT.wait([dg_free[k]])
                    t_conv[j] = ACT.op(nc.scalar.activation(out=yc[:, j, :], in_=banks[b][:, 0:TT], func=AF.Identity,
                                                            bias=cs[:, C_CB + j:C_CB + j + 1], scale=1.0))
                    bank_free[b] = t_conv[j]
                c.t_conv = t_conv
                c.t_convmm = dg_free[k]

            t_q = []
            t_krot = [None, None]
            pend = []

            def rope2(cidx, zrb, zb, t_z, t_a):
                (b2,) = acquire(1)
                PE.wait([t_z])
                t_p = PE.op(nc.tensor.matmul(banks[b2][:, 0:tt], pmat[:], zb[:, 0:tt], start=True, stop=True))
                rope_zb_free[cidx % 2] = t_p
                DVE.wait([t_p, c.t_cs, state.get("ysq_rd1")])
                t_s2 = DVE.op(nc.vector.tensor_tensor(out=zq[:, 0:tt], in0=banks[b2][:, 0:tt], in1=cst_t[:, 1, 0:tt],
                                                      op=ALU.mult))
                bank_free[b2] = t_s2
                DVE.wait([t_s2, t_a])
                t_r = DVE.op(nc.vector.tensor_tensor(out=zrb[:, 0:tt], in0=zrb[:, 0:tt], in1=zq[:, 0:tt],
                                                     op=ALU.add))
                ACT.wait([t_r, a_ok, t_h])
                if cidx < 8:
                    t_w = ACT.op(nc.scalar.copy(out=c.qT[:, cidx, 0:tt], in_=zrb[:, 0:tt]))
                elif sample:
                    t_w = ACT.op(nc.scalar.copy(out=krot_s[:, cidx - 8, :], in_=zrb[:, 0:tt]))
                else:
                    ACT.op(nc.scalar.copy(out=kTa[0:64, cidx - 8, 128:128 + tt], in_=zrb[0:64, 0:tt]))
                    t_w = ACT.op(nc.scalar.copy(out=kTb[64:128, cidx - 8, 128:128 + tt], in_=zrb[64:128, 0:tt]))
                t_q.append(t_w)
                rope_zr_free[cidx % 2] = t_w
                if cidx >= 8 and last:
                    (b3,) = acquire(1)
                    PE.wait([t_r])
                    t_t = PE.op(nc.tensor.transpose(banks[b3][:, 0:128], zrb[:, 384:512], ident))
                    DVE.wait([t_t, state.get("kout_rd"), state.get("ln_rd")])
                    t_ko = DVE.op(nc.vector.tensor_copy(kout[:, (cidx - 8) * 128:(cidx - 7) * 128],
                                                        banks[b3][:, 0:128]))
                    bank_free[b3] = t_ko
                    t_krot[cidx - 8] = t_ko
                    rope_zr_free[cidx % 2] = t_ko

            for t in range(5):
                wi, wv = w_get()
                for j in range(2):
                    cidx = t * 2 + j
                    b, tm = fm_chunk(c, wv, j)
                    if j == 1:
                        w_done(wi, tm)
                    zb = zqb2[cidx % 2]
                    zrb = zr[cidx % 2]
                    ACT.wait([tm, rope_zb_free[cidx % 2]])
                    t_z = ACT.op(nc.scalar.activation(out=zb[:, 0:tt], in_=banks[b][:, 0:tt], func=AF.Identity,
                                                      bias=cs[:, C_BFM + 16 + cidx:C_BFM + 17 + cidx], scale=1.0))
                    DVE.wait([tm, t_z, c.t_cs, rope_zr_free[cidx % 2]])
                    t_a = DVE.op(nc.vector.scalar_tensor_tensor(
                        out=zrb[:, 0:tt], in0=banks[b][:, 0:tt], scalar=cs[:, C_BFM + 16 + cidx:C_BFM + 17 + cidx],
                        in1=cst_t[:, 0, 0:tt], op0=ALU.add, op1=ALU.mult))
                    bank_free[b] = t_a
                    pend.append((cidx, zrb, zb, t_z, t_a))
                    if len(pend) > 1:
                        rope2(*pend.pop(0))
            while pend:
                rope2(*pend.pop(0))
            if last:
                SP.wait(t_krot)
                state["kout_rd"] = dma(SP, sem_m, o_pk[c.seq, :, :], kout)
            c.t_q = t_q

        def ph_B2(c):
            tt, gs, ng, sample, last = c.tt, c.gs, c.ng, c.sample, c.last
            a_ok = state["abuf_rd"]
            t_h = c.t_h
            wi, wv = w_get()
            t_v = []
            for g in range(ng):
                (b,) = acquire(1)
                for kc in range(16):
                    ins = nc.tensor.matmul(banks[b][0:gs, 0:256], c.xnT[:, kc, g * gs:(g + 1) * gs], wv[:, kc, 0:256],
                                           start=(kc == 0), stop=(kc == 15))
                tm = PE.op(ins)
                vb = v32[g % 2]
                DVE.wait([tm, state.get("v32_rd%d" % (g % 2))])
                t1 = DVE.op(nc.vector.tensor_tensor(out=vb[0:gs, :], in0=banks[b][0:gs, 0:256], in1=bvb[0:gs, :],
                                                    op=ALU.add))
                bank_free[b] = t1
                if not sample:
                    ACT.wait([t1, a_ok, t_h])
                    t2 = ACT.op(nc.scalar.copy(out=Vt[:, 1 + g, :], in_=vb[:, :]))
                    t_v.append(t2)
                    state["v32_rd%d" % (g % 2)] = t2
                    if last and g == 3:
                        SP.wait([t1])
                        state["v32_rd1"] = dma(SP, sem_m, o_pv[c.seq, :, :], vb[:, :])
                else:
                    t_v.append(t1)
            w_done(wi, tm)
            tm_last_win[0] = tm
            c.t_v = t_v

        def ph_LN(c, split=False):
            tt, sample, last = c.tt, c.sample, c.last
            t_u = c.t_u
            if not sample:
                DVE.wait([c.t_convmm])
                c.t_uh = DVE.op(nc.vector.tensor_copy(uhalo[:], uext[:, :, TT:TT + 30]))
                if last:
                    (b3, b4) = acquire(2)
                    PE.wait(t_u)
                    for j in range(8):
                        bk = banks[b3] if j < 4 else banks[b4]
                        ins = nc.tensor.transpose(bk[0:30, (j % 4) * 128:(j % 4 + 1) * 128], ulast[:, j, :], ident)
                    t_t = PE.op(ins)
                    DVE.wait([t_t, state.get("ucout_rd")])
                    DVE.op(nc.vector.tensor_copy(ucout[0:30, 0:512], banks[b3][0:30, :]))
                    t_uc = DVE.op(nc.vector.tensor_copy(ucout[0:30, 512:1024], banks[b4][0:30, :]))
                    bank_free[b3] = t_uc
                    bank_free[b4] = t_uc
                    SP.wait([t_uc])
                    state["ucout_rd"] = dma(SP, sem_m, o_pc[c.seq, :, :], ucout[0:30, :])
                t_conv = c.t_conv
            else:
                t_conv = sample_conv(t_u)
                c.t_conv = t_conv
                c.t_uh = None

            c.t_conv_l = t_conv
            if not split:
                ln_stats(c)
                ln_mid(c)
                ln_rs(c)
                ln_chunks(c, 0, 8)

        def ln_stats(c):
            tt = c.tt
            t_conv = c.t_conv_l
            (bs1, bs2) = acquire(2)
            c.bs = (bs1, bs2)
            for j in range(8):
                yq = ysq[j % 2]
                ACT.wait([t_conv[j], state.get("ysq_rd%d" % (j % 2))])
                t_s = ACT.op(nc.scalar.activation(out=yq[:, 0:tt], in_=c.yc[:, j, 0:tt], func=AF.Square))
                PE.wait([t_conv[j], t_s])
                nc.tensor.matmul(banks[bs1][:, 0:tt], ones, c.yc[:, j, 0:tt], start=(j == 0), stop=(j == 7))
                state["ysq_rd%d" % (j % 2)] = PE.op(
                    nc.tensor.matmul(banks[bs2][:, 0:tt], ones, yq[:, 0:tt], start=(j == 0), stop=(j == 7)))
            c.t_st = state["ysq_rd1"]

        def ln_mid(c):
            tt = c.tt
            bs1, bs2 = c.bs
            t_st = c.t_st
            ACT.wait([t_st, state.get("ln_rd"), state.get("kout_rd")])
            t_mu = ACT.op(nc.scalar.activation(out=mu[:, 0:tt], in_=banks[bs1][:, 0:tt], func=AF.Copy, scale=1.0 / 1024))
            DVE.wait([t_mu, t_st, state.get("ln_rd")])
            t_1 = DVE.op(nc.vector.tensor_tensor(out=rs_ln[:, 0:tt], in0=mu[:, 0:tt], in1=mu[:, 0:tt], op=ALU.mult))
            DVE.wait([t_1])
            t_2 = DVE.op(nc.vector.scalar_tensor_tensor(out=rs_ln[:, 0:tt], in0=banks[bs2][:, 0:tt], scalar=1.0 / 1024,
                                                        in1=rs_ln[:, 0:tt], op0=ALU.mult, op1=ALU.subtract))
            bank_free[bs1] = t_2
            bank_free[bs2] = t_2
            c.t_2 = t_2

        def ln_rs(c):
            tt = c.tt
            ACT.wait([c.t_2])
            t_3 = ACT.op(nc.scalar.activation(out=rs_ln[:, 0:tt], in_=rs_ln[:, 0:tt], func=AF.Sqrt, scale=1.0,
                                              bias=cs[:, C_EPS:C_EPS + 1]))
            DVE.wait([t_3])
            c.t_4 = DVE.op(nc.vector.reciprocal(rs_ln[:, 0:tt], rs_ln[:, 0:tt]))
            c.t_co = []

        def ln_chunks(c, j0, j1):
            tt = c.tt
            t_conv = c.t_conv_l
            for j in range(j0, j1):
                DVE.wait([c.t_4, t_conv[j]])
                t_a = DVE.op(nc.vector.tensor_tensor(out=c.yc[:, j, 0:tt], in0=c.yc[:, j, 0:tt], in1=mu[:, 0:tt],
                                                     op=ALU.subtract))
                DVE.wait([t_a])
                t_b = DVE.op(nc.vector.tensor_tensor(out=c.yc[:, j, 0:tt], in0=c.yc[:, j, 0:tt], in1=rs_ln[:, 0:tt],
                                                     op=ALU.mult))
                ACT.wait([t_b, state["mix_rd"]])
                c.t_co.append(ACT.op(nc.scalar.activation(out=c.mixT[:, 8 + j, 0:tt], in_=c.yc[:, j, 0:tt],
                                                          func=AF.Silu, scale=cs[:, C_LG + j:C_LG + j + 1],
                                                          bias=cs[:, C_LB + j:C_LB + j + 1])))
                state["ln_rd"] = t_b
                state["yc_rd"] = c.t_co[-1]

        def ph_LNF(n, c):
            gs = c.gs

            def src_h(g, xb):
                ACT.wait([c.t_e, xb_free[g % 2], xb_st[g % 2]])
                return c.t_e, c.hres[0:gs, g, :]

            ph_LN(n, split=True)
            ln_stats(n)
            c.f3 = {}
            c.f3[0] = norm_pre(c, 0, src_h)
            c.f3[1] = norm_pre(c, 1, src_h)
            ln_mid(n)
            tF = norm_post(c, 0, c.f3[0], C_G2, c.xnT, tm_last_win[0])
            c.f3[2] = norm_pre(c, 2, src_h)
            ln_rs(n)
            ln_chunks(n, 0, 4)
            tF += norm_post(c, 1, c.f3[1], C_G2, c.xnT, tm_last_win[0])
            c.f3[3] = norm_pre(c, 3, src_h)
            ln_chunks(n, 4, 8)
            tF += norm_post(c, 2, c.f3[2], C_G2, c.xnT, tm_last_win[0])
            tF += norm_post(c, 3, c.f3[3], C_G2, c.xnT, tm_last_win[0])
            PE.wait(tF)

        def ph_F_pre01(c):
            gs = c.gs

            def src_h(g, xb):
                ACT.wait([c.t_e, xb_free[g % 2], xb_st[g % 2]])
                return c.t_e, c.hres[0:gs, g, :]

            c.f3 = {}
            c.f3[0] = norm_pre(c, 0, src_h)
            c.f3[1] = norm_pre(c, 1, src_h)

        def attn_gen(c):
            if c.sample:
                c.t_att = sample_attention(c.t_q, c.t_v)
                c.t_kh = None
                return
            first_seq_tile = c.first
            t_q, t_v = c.t_q, c.t_v
            items = [(g, cc_, b) for g in range(4) for cc_ in range(8) for b in range(2)]
            n = len(items)
            stA = {}
            outs = []
            a_free = state.setdefault("a_free", {})

            def s1(i):
                g, cc_, b = items[i]
                slot = cc_ * 2 + b
                col = i % 64
                u = i % NPB
                bS = 4 + (i % 2)
                PE.wait(t_q + t_v + [t_k8, a_free.get(("S", i % 2))])
                mi = 1 if (first_seq_tile and g == 0) else 0
                nc.tensor.matmul(banks[bS][:, 256:257], identb[:], skh[:, slot:slot + 1], start=True, stop=False)
                nc.tensor.matmul(banks[bS][:, 256:257], identb[:], skl[:, slot:slot + 1], start=False, stop=True)
                kTx = kTa if b == 0 else kTb
                nc.tensor.matmul(banks[bS][:, 0:256], qT[:, cc_, g * 128:(g + 1) * 128],
                                 kTx[:, cc_ // 4, g * 128:g * 128 + 256], start=True, stop=False)
                t_s = PE.op(nc.tensor.matmul(banks[bS][:, 0:256], identb[:], maskb[:, mi, :], start=False, stop=True))
                DVE.wait([t_s])
                t_m = DVE.op(nc.vector.reduce_max(sm_mx[:, col:col + 1], banks[bS][:, 0:257], AX.X))
                DVE.wait([t_m])
                t_nb = DVE.op(nc.vector.tensor_scalar(sm_nb[:, col:col + 1], sm_mx[:, col:col + 1], -0.125, None,
                                                      ALU.mult))
                ACT.wait([t_nb, a_free.get(("pb", u))])
                t_p = ACT.op(nc.scalar.activation(out=pb[u][:, 0:257], in_=banks[bS][:, 0:257], func=AF.Exp,
                                                  scale=0.125, bias=sm_nb[:, col:col + 1],
                                                  accum_out=sm_rs[:, col:col + 1]))
                a_free[("S", i % 2)] = t_p
                stA[("t_p", i)] = t_p

            def s2(i):
                col = i % 64
                u = i % NPB
                v = i % NPN
                DVE.wait([stA[("t_p", i)]])
                t_r = DVE.op(nc.vector.reciprocal(sm_rd[:, col:col + 1], sm_rs[:, col:col + 1]))
                ACT.wait([t_r, a_free.get(("pn", v))])
                t_n = ACT.op(nc.scalar.activation(out=pn[v][:], in_=pb[u][:, 0:256], func=AF.Copy,
                                                  scale=sm_rd[:, col:col + 1]))
                a_free[("pb", u)] = t_n
                stA[("t_n", i)] = t_n

            def s3a(i):
                v = i % NPN
                w = i % NPT
                qs = 0
                PE.wait([stA[("t_n", i)], a_free.get(("PT", qs))])
                ptv = banks[6][:].bitcast(BF16)[:, qs * 256:(qs + 1) * 256]
                nc.tensor.transpose(ptv[:, 0:128], pn[v][:, 0:128], identb[:])
                t_t = PE.op(nc.tensor.transpose(ptv[:, 128:256], pn[v][:, 128:256], identb[:]))
                a_free[("pn", v)] = t_t
                DVE.wait([t_t, a_free.get(("pts", w))])
                t_c2 = DVE.op(nc.vector.tensor_copy(pts[w][:], ptv))
                a_free[("PT", qs)] = t_c2
                stA[("t_c2", i)] = t_c2

            def s3b(i):
                g, cc_, b = items[i]
                w = i % NPT
                gkv = 2 * (cc_ // 4) + b
                ov = banks[7][:, 0:128]
                PE.wait([stA[("t_c2", i)], a_free.get(("O", 0)), a_free.get(("O", 1))])
                nc.tensor.matmul(ov[b * 64:(b + 1) * 64, :], Vt[:, g, gkv * 64:(gkv + 1) * 64],
                                 pts[w][:, 0:128], start=True, stop=False, skip_group_check=True)
                t_o = PE.op(nc.tensor.matmul(ov[b * 64:(b + 1) * 64, :], Vt[:, g + 1, gkv * 64:(gkv + 1) * 64],
                                             pts[w][:, 128:256], start=False, stop=True, skip_group_check=True))
                a_free[("pts", w)] = t_o
                ACT.wait([t_o, state["mix_rd"]])
                t_e = ACT.op(nc.scalar.copy(out=mixT_v[b * 64:(b + 1) * 64, cc_, g * 128:(g + 1) * 128],
                                            in_=ov[b * 64:(b + 1) * 64, :]))
                a_free[("O", b)] = t_e
                outs.append(t_e)

            K2, K3, K4 = 2, 4, 6
            for step in range(n + K4):
                if step < n:
                    s1(step)
                if 0 <= step - K2 < n:
                    s2(step - K2)
                if 0 <= step - K3 < n:
                    s3a(step - K3)
                if 0 <= step - K4 < n:
                    s3b(step - K4)
                yield
            DVE.wait(t_q + t_v)
            DVE.op(nc.vector.tensor_copy(khalo[0:64, :, :], kTa[0:64, :, TT:TT + 128]))
            DVE.op(nc.vector.tensor_copy(khalo[64:128, :, :], kTb[64:128, :, TT:TT + 128]))
            c.t_kh = DVE.op(nc.vector.tensor_copy(vhalo[:], Vt[:, 4, :]))
            c.t_att = outs

        def weave(gen, k):
            if gen is None:
                return
            for _ in range(k):
                try:
                    next(gen)
                except StopIteration:
                    return

        def drain(gen):
            if gen is None:
                return
            for _ in gen:
                pass

        def tm_gemm(c, src_T, gen, kweave, acc_fixed=None, acc_wait=()):
            ng, gs = c.ng, c.gs
            wi, wv = w_get()
            nb = 2 if ng == 4 else 1
            if acc_fixed is None:
                acc = acquire(nb)
            else:
                acc = acc_fixed
                PE.wait(list(acc_wait))
            for kc in range(16):
                for g in range(ng):
                    ins = nc.tensor.matmul(banks[acc[g // 2]][0:gs, (g % 2) * 256:(g % 2) * 256 + 256],
                                           src_T[:, kc, g * gs:(g + 1) * gs], wv[:, kc, 0:256],
                                           start=(kc == 0 and g % 2 == 0), stop=(kc == 15), skip_group_check=True)
            tm = PE.op(ins)
            w_done(wi, tm)
            if kweave:
                weave(gen, 1)
            return acc, tm

        def prep_x(c):
            gs = c.gs
            par = c.ti % 2
            POOL.wait(state.get("xbs_rd%d" % par) or [])
            c.t_prep = [None] * 4
            for g in range(c.ng):
                dst = xbs[par, g * gs:(g + 1) * gs, :]
                t1 = dma(POOL, sem_pb, dst, c.xsrc[c.r0 + g * gs:c.r0 + (g + 1) * gs, :])
                POOL.wait([t1])
                c.t_prep[g] = dma(POOL, sem_pb, dst, bo.partition_broadcast(128)[0:gs, :], accum=ALU.add)

        def ph_E(c, genA=None):
            ng, gs, sample = c.ng, c.gs, c.sample
            lst = state.get("hres_free") or [None] * 4
            t_x = [None] * 4
            par = c.ti % 2
            for g in range(ng):
                SP.wait(([lst[g]] if not sample else lst) + [c.t_prep[g]])
                t_x[g] = dma(SP, sem_hg[g], c.hres[0:gs, g, :], xbs[par, g * gs:(g + 1) * gs, :])
            c.t_xrl = t_x
            PE.wait(c.t_att + c.t_co)
            weave(genA, 1)
            for d in range(8):
                if d == 1:
                    weave(genA, 1)
                af = state.setdefault("a_free", {})
                if ng == 4 and d == 2:
                    acc, tm = tm_gemm(c, c.mixT, None, 0, acc_fixed=[4, 5],
                                      acc_wait=[af.get(("S", 0)), af.get(("S", 1))])
                elif ng == 4 and d == 3:
                    acc, tm = tm_gemm(c, c.mixT, None, 0, acc_fixed=[6, 7],
                                      acc_wait=[af.get(("PT", 0)), af.get(("O", 0)), af.get(("O", 1))])
                else:
                    acc, tm = tm_gemm(c, c.mixT, None, 0)
                for g in range(ng):
                    DVE.wait([tm, t_x[g]])
                    hv = c.hres[0:gs, g, d * 256:(d + 1) * 256]
                    t_e = DVE.op(nc.vector.tensor_tensor(
                        out=hv, in0=banks[acc[g // 2]][0:gs, (g % 2) * 256:(g % 2) * 256 + 256], in1=hv, op=ALU.add))
                    if acc[0] < 4:
                        bank_free[acc[g // 2]] = t_e
                if ng == 4 and d == 2:
                    af[("S", 0)] = t_e
                    af[("S", 1)] = t_e
                elif ng == 4 and d == 3:
                    af[("PT", 0)] = t_e
                    af[("O", 0)] = t_e
                    af[("O", 1)] = t_e
                if d in (3, 5, 6, 7):
                    weave(genA, 1)
            state["mix_rd"] = tm
            state["xbs_rd%d" % (c.ti % 2)] = [x_ for x_ in c.t_xrl if x_ is not None]
            c.t_e = t_e
            drain(genA)

        def ph_F(c):
            gs = c.gs

            def src_h(g, xb):
                ACT.wait([c.t_e, xb_free[g % 2], xb_st[g % 2]])
                return c.t_e, c.hres[0:gs, g, :]

            tF = norm_T(c, src_h, C_G2, c.xnT, tm_last_win[0])
            PE.wait(tF)

        def ph_G(c, gen):
            tt, ng, gs = c.tt, c.ng, c.gs
            ri = 0
            for q in range(4):
                t_h2 = []
                for t in range(8):
                    wi, wv = w_get()
                    for j in range(2):
                        b, tm = fm_chunk(c, wv, j, gen)
                        if j == 1:
                            w_done(wi, tm)
                        rix = ri % 2
                        rb = rl[rix]
                        ri += 1
                        ACT.wait([tm, state.get("rl_rd%d" % rix)])
                        t_r = ACT.op(nc.scalar.activation(out=rb[:, 0:tt], in_=banks[b][:, 0:tt], func=AF.Relu))
                        bank_free[b] = t_r
                        DVE.wait([t_r, state["ych_rd"], state.get("yc_rd")])
                        t_s = DVE.op(nc.vector.tensor_tensor(out=c.h2T[:, t * 2 + j, 0:tt], in0=rb[:, 0:tt],
                                                             in1=rb[:, 0:tt], op=ALU.mult))
                        state["rl_rd%d" % rix] = t_s
                        t_h2.append(t_s)
                        weave(gen, 1)
                tm_last_up = tm
                PE.wait(t_h2)
                for d in range(8):
                    acc, tm = tm_gemm(c, c.h2T, gen, 4)
                    DVE.wait([tm])
                    for g in range(ng):
                        hv = c.hres[0:gs, g, d * 256:(d + 1) * 256]
                        t_e = DVE.op(nc.vector.tensor_tensor(
                            out=hv, in0=banks[acc[g // 2]][0:gs, (g % 2) * 256:(g % 2) * 256 + 256], in1=hv,
                            op=ALU.add))
                        bank_free[acc[g // 2]] = t_e
                state["ych_rd"] = tm
            state["xn_rd"] = tm_last_up
            c.t_e = t_e
            drain(gen)

        def ph_H(c):
            gs = c.gs
            hfree = [None] * 4
            for g in range(c.ng):
                xb = xsb[g % 2]
                ACT.wait([c.t_e])
                ACT.op(nc.scalar.activation(out=sqj[0:gs, :], in_=c.hres[0:gs, g, :], func=AF.Square,
                                            accum_out=ss[0:gs, g:g + 1]))
                t1 = ACT.op(nc.scalar.activation(out=ss[0:gs, 8 + g:9 + g], in_=ss[0:gs, g:g + 1], func=AF.Sqrt,
                                                 scale=1.0 / D, bias=cs[0:gs, C_EPS:C_EPS + 1]))
                DVE.wait([t1])
                t2 = DVE.op(nc.vector.reciprocal(rstd[0:gs, g:g + 1], ss[0:gs, 8 + g:9 + g]))
                DVE.wait([t2, xb_free[g % 2], xb_st[g % 2]])
                t3 = DVE.op(nc.vector.scalar_tensor_tensor(out=xb[0:gs, :], in0=c.hres[0:gs, g, :],
                                                           scalar=rstd[0:gs, g:g + 1], in1=gfb[0:gs, :],
                                                           op0=ALU.mult, op1=ALU.mult))
                hfree[g] = t3
                SP.wait([t3])
                xb_st[g % 2] = dma(SP, sem_xo[g % 2], c.ydst[c.r0 + g * gs:c.r0 + (g + 1) * gs, :], xb[0:gs, :])
            state["hres_free"] = hfree

        def finish_attn(c):
            DVE.wait(c.t_att + [c.t_uh, c.t_kh] + list(c.t_conv))
            state["abuf_rd"] = DVE.op(nc.vector.memset(ss[:, 15:16], 0.0))

        def sample_conv(t_u):
            SP.wait([state["xn_rd"]] + (state.get("hres_free") or []))
            for a in range(4):
                t_l = dma(SP, sem_m, ccst[:, a, :], cc[4 * a:4 * a + 4, :, :].rearrange("s w c -> (s w) c"))
            PE.wait([t_l])
            tcp = []
            for a in range(4):
                for j in range(8):
                    (b,) = acquire(1)
                    t_t = PE.op(nc.tensor.transpose(banks[b][:, 0:120], ccst[:, a, j * 128:(j + 1) * 128],
                                                    ident[0:120, 0:120]))
                    DVE.wait([t_t, state["xn_rd"]])
                    t_c = DVE.op(nc.vector.tensor_copy(
                        ccT[:, j, 4 * a:4 * a + 4, :],
                        banks[b][:, 0:120].rearrange("p (s w) -> p s w", s=4)))
                    bank_free[b] = t_c
                    tcp.append(t_c)
            cwv = cs[:, C_CW:C_CW + 248].rearrange("p (j w) -> p j w", j=8)
            DVE.wait(tcp)
            t_2 = None
            for hf in range(2):
                DVE.wait([t_2])
                t_1 = DVE.op(nc.vector.tensor_tensor(
                    out=prod_h, in0=ccT[:, 4 * hf:4 * hf + 4, :, :],
                    in1=cwv[:, 4 * hf:4 * hf + 4, 0:30].unsqueeze(2).to_broadcast([128, 4, NS, 30]), op=ALU.mult))
                DVE.wait([t_1])
                t_2 = DVE.op(nc.vector.reduce_sum(yc_s[:, 4 * hf:4 * hf + 4, :], prod_h, AX.X))
            DVE.wait([t_2] + list(t_u))
            t_3 = DVE.op(nc.vector.tensor_tensor(out=prod_u, in0=uext_s[:, :, 30:30 + NS],
                                                 in1=cwv[:, :, 30:31].to_broadcast([128, 8, NS]), op=ALU.mult))
            DVE.wait([t_3])
            t_4 = DVE.op(nc.vector.tensor_tensor(out=yc_s[:], in0=yc_s[:], in1=prod_u, op=ALU.add))
            DVE.wait([t_4])
            t_5 = DVE.op(nc.vector.tensor_tensor(out=yc_s[:], in0=yc_s[:],
                                                 in1=cs[:, C_CB:C_CB + 8].unsqueeze(2).to_broadcast([128, 8, NS]),
                                                 op=ALU.add))
            (b,) = acquire(1)
            (b2,) = acquire(1)
            PE.wait(list(t_u))
            for j in range(8):
                bk = banks[b] if j < 4 else banks[b2]
                ins = nc.tensor.transpose(bk[0:NS, (j % 4) * 128:(j % 4 + 1) * 128], uext_s[:, j, 30:30 + NS], ident)
            t_t = PE.op(ins)
            DVE.wait([t_t])
            DVE.wait([state.get("ucout_rd")])
            DVE.op(nc.vector.tensor_copy(utm_s[:, 0:512], banks[b][0:NS, :]))
            t_c = DVE.op(nc.vector.tensor_copy(utm_s[:, 512:1024], banks[b2][0:NS, :]))
            bank_free[b] = t_c
            bank_free[b2] = t_c
            SP.wait([t_c])
            dma(SP, sem_o, o_sc[:, 29, :], utm_s)
            return [t_5] * 8

        def sample_attention(t_q, t_v):
            (b,) = acquire(1)
            PE.wait(t_q)
            for c2 in range(2):
                ins = nc.tensor.transpose(banks[b][0:NS, c2 * 128:(c2 + 1) * 128], krot_s[:, c2, :], ident)
            t_t = PE.op(ins)
            DVE.wait([t_t])
            t_k = DVE.op(nc.vector.tensor_copy(ktm_s[:], banks[b][0:NS, 0:256]))
            bank_free[b] = t_k
            SP.wait([t_k] + t_v)
            t_r1 = dma(SP, sem_sk, o_sk[:, 127, :], ktm_s[:])
            t_r2 = dma(SP, sem_sk, o_sv[:, 127, :], v32[0][0:NS, :])
            SP.wait([t_r1, t_r2, early_tok[0], state["ych_rd"], state["mix_rd"]])
            t_l = dma(SP, sem_m, Kc[:, :, :], o_sk.rearrange("s k d -> k s d"))
            t_l2 = dma(SP, sem_m, Vc[:, :, :], o_sv.rearrange("s k d -> k s d"))
            t_vb = None
            PE.wait([t_l2, t_l, xb_free_ref[1]])
            DVE.wait([xb_st[1]])
            tk = []
            for s in range(NS):
                (b,) = acquire(1)
                for c2 in range(2):
                    ins = nc.tensor.transpose(banks[b][:, c2 * 128:(c2 + 1) * 128], Kc[:, s, c2 * 128:(c2 + 1) * 128],
                                              ident)
                t_t = PE.op(ins)
                DVE.wait([t_t])
                t_c = DVE.op(nc.vector.tensor_copy(knT[:, :, s, :],
                                                   banks[b][:, 0:256].rearrange("p (c k) -> p c k", c=2)))
                bank_free[b] = t_c
                tk.append(t_c)
            (bS,) = acquire(1)
            PE.wait(tk)
            stv = banks[bS][:, 0:256].rearrange("p (c b s) -> p c b s", c=8, b=2)
            for s in range(NS):
                for gkv in range(4):
                    b_ = gkv % 2
                    c0 = 4 * (gkv // 2)
                    ins = nc.tensor.matmul(stv[:, c0:c0 + 4, b_, s], knT[b_ * 64:(b_ + 1) * 64, gkv // 2, s, :],
                                           qT_s[b_ * 64:(b_ + 1) * 64, c0:c0 + 4, s], start=True, stop=True)
            t_s = PE.op(ins)
            DVE.wait([t_s, state.get("ysq_rd1")])
            t_c = DVE.op(nc.vector.tensor_copy(st_sb[:], banks[bS][:, 0:256]))
            bank_free[bS] = t_c
            (b2,) = acquire(1)
            PE.wait([t_c])
            nc.tensor.transpose(banks[b2][:, 0:128], st_sb[:, 0:128], ident)
            t_t = PE.op(nc.tensor.transpose(banks[b2][:, 128:256], st_sb[:, 128:256], ident))
            sk_s = sm_mx[:, 0:2]
            SP.wait([t_k8])
            for hh in range(2):
                for sl in range(8):
                    slot = hh * 8 + sl
                    t_sk = dma(SP, sem_m, sk_s[sl * NS:(sl + 1) * NS, hh:hh + 1],
                               cst[0:NS, C_SK + slot:C_SK + slot + 1], slow=True)
            DVE.wait([t_t, t_sk])
            s2 = banks[b2][:, 0:256].rearrange("p (h k) -> p h k", h=2)
            DVE.op(nc.vector.reduce_max(sm_mx[:, 2:4], s2, AX.X))
            DVE.wait([(DVE.sem, DVE.sem.v)])
            DVE.op(nc.vector.tensor_scalar(sm_mx[:, 4:6], sk_s, 8.0, None, ALU.mult))
            DVE.wait([(DVE.sem, DVE.sem.v)])
            DVE.op(nc.vector.tensor_tensor(out=sm_mx[:, 2:4], in0=sm_mx[:, 2:4], in1=sm_mx[:, 4:6], op=ALU.max))
            DVE.wait([(DVE.sem, DVE.sem.v)])
            t_nb = DVE.op(nc.vector.tensor_scalar(sm_nb[:, 0:2], sm_mx[:, 2:4], -0.125, None, ALU.mult))
            ACT.wait([t_nb])
            for hh in range(2):
                ACT.op(nc.scalar.activation(out=pb[hh][:, 0:128], in_=s2[:, hh, :], func=AF.Exp, scale=0.125,
                                            bias=sm_nb[:, hh:hh + 1], accum_out=sm_rs[:, hh:hh + 1]))
                t_e = ACT.op(nc.scalar.activation(out=sm_es[:, hh:hh + 1], in_=sk_s[:, hh:hh + 1], func=AF.Exp,
                                                  scale=1.0, bias=sm_nb[:, hh:hh + 1]))
            bank_free[b2] = t_e
            DVE.wait([t_e])
            DVE.op(nc.vector.tensor_tensor(out=sm_rd[:, 0:2], in0=sm_rs[:, 0:2], in1=sm_es[:, 0:2], op=ALU.add))
            DVE.wait([(DVE.sem, DVE.sem.v)])
            t_r = DVE.op(nc.vector.reciprocal(sm_rd[:, 0:2], sm_rd[:, 0:2]))
            ACT.wait([t_r])
            for hh in range(2):
                t_n = ACT.op(nc.scalar.activation(out=pn_s[:, hh, :], in_=pb[hh][:, 0:128], func=AF.Copy,
                                                  scale=sm_rd[:, hh:hh + 1]))
            (b3,) = acquire(1)
            PE.wait([t_n])
            ptv = banks[b3]
            nc.tensor.transpose(ptv[:, 0:128], pn_s[:, 0, :], ident)
            t_t = PE.op(nc.tensor.transpose(ptv[:, 128:256], pn_s[:, 1, :], ident))
            ACT.wait([t_t])
            t_c = ACT.op(nc.scalar.copy(out=pt_s[:], in_=ptv[:, 0:256]))
            bank_free[b3] = t_c
            (bO,) = acquire(1)
            PE.wait([t_c, t_vb])
            ptv2 = pt_s[:].rearrange("p (c b s) -> p c b s", c=8, b=2)
            ov = banks[bO][:, 0:8 * NS].rearrange("p (c s) -> p c s", c=8)
            for s in range(NS):
                for gkv in range(4):
                    b_ = gkv % 2
                    c0 = 4 * (gkv // 2)
                    ins = nc.tensor.matmul(ov[b_ * 64:(b_ + 1) * 64, c0:c0 + 4, s],
                                           Vc[:, s, gkv * 64:(gkv + 1) * 64], ptv2[:, c0:c0 + 4, b_, s],
                                           start=True, stop=True)
            t_o = PE.op(ins)
            ACT.wait([t_o])
            t_e = ACT.op(nc.scalar.copy(out=mixT_s[:, 0:8, :], in_=ov))
            bank_free[bO] = t_e
            return [t_e]

        ctxs = [make_ctx(i) for i in range(NTILE + 1)]
        c0 = ctxs[0]
        prep_x(c0)
        ph_A(c0)
        t_gf = dma(SP, sem_c, gfb[:], gf.partition_broadcast(128))
        DVE.wait([t_gf])
        dma(SP, sem_o, o_sk[:, 0:127, :], ck[:, 1:128, :])
        dma(SP, sem_o, o_sv[:, 0:127, :], cv[:, 1:128, :])
        early_tok[0] = dma(SP, sem_o, o_sc[:, 0:29, :], cc[:, 1:30, :])
        ph_B(c0)
        ph_LN(c0)
        drain(attn_gen(c0))
        finish_attn(c0)
        prep_x(ctxs[1])
        ph_E(c0, ph_A_gen(ctxs[1]))
        for i in range(NTILE):
            c = ctxs[i]
            gen = None
            n_ = None
            if i + 1 < NTILE:
                n_ = ctxs[i + 1]
                ph_B1(n_)
                ph_B2(n_)
                ph_LNF(n_, c)
                gen = attn_gen(n_)
            else:
                ph_F(c)
            if i + 2 <= NTILE:
                prep_x(ctxs[i + 2])
            ph_G(c, gen)
            if n_ is not None:
                finish_attn(n_)
            ph_H(c)
            if n_ is not None:
                ph_E(n_, ph_A_gen(ctxs[i + 2]) if i + 2 < NTILE else None)
        s_ = ctxs[NTILE]
        ph_A(s_)
        ph_B(s_)
        ph_LN(s_)
        drain(attn_gen(s_))
        ph_E(s_)
        ph_F(s_)
        ph_G(s_, None)
        ph_H(s_)

        SP.wait([(sem_o, sem_o.v), (sem_m, sem_m.v), (sem_sk, sem_sk.v)] + [(x_, x_.v) for x_ in sem_xo])
    return nc


_NC_CACHE = {}


def _host_consts(inputs):
    f32 = np.float32
    cst = np.zeros((128, C_END), f32)
    cst[:, C_ID:C_ID + 128] = np.eye(128, dtype=f32)
    pm = np.zeros((128, 128), f32)
    for m in range(128):
        r = m % 64
        if r < 8:
            pm[m + 8, m] = 1.0
        elif r < 16:
            pm[m - 8, m] = 1.0
    cst[:, C_PM:C_PM + 128] = pm
    cst[:, C_ONE:C_ONE + 128] = 1.0
    i = np.arange(128)[:, None]
    j = np.arange(256)[None, :]
    mk = np.where(j < 128, j > i, (j - 128) <= i).astype(f32)
    cst[:, C_MK:C_MK + 256] = mk
    mk0 = mk.copy()
    mk0[:, 0:128] = 0.0
    cst[:, C_MK0:C_MK0 + 256] = mk0
    cst[:, C_G1:C_G1 + 16] = inputs["norm1_g"][0].reshape(16, 128).T
    cst[:, C_G2:C_G2 + 16] = inputs["norm2_g"][0].reshape(16, 128).T
    cst[:, C_CB:C_CB + 8] = inputs["conv_b"][0].reshape(8, 128).T
    cst[:, C_LG:C_LG + 8] = inputs["conv_ln_g"][0].reshape(8, 128).T
    cst[:, C_LB:C_LB + 8] = inputs["conv_ln_b"][0].reshape(8, 128).T
    cw = inputs["conv_w"][0]
    cst[:, C_CW:C_CW + 248] = cw.T.reshape(8, 128, 31).transpose(1, 0, 2).reshape(128, 248)
    return cst


def _col_perm():
    cols = []
    for jj in range(8):
        cols.append(np.arange(2560 + jj * 128, 2560 + (jj + 1) * 128))
        cols.append(np.arange(1536 + jj * 128, 1536 + (jj + 1) * 128))
    for c in range(8):
        for b in range(2):
            h = head_of(c, b)
            cols.append(np.arange(h * 64, (h + 1) * 64))
    cols.append(np.arange(1024, 1280))
    cols.append(np.arange(1280, 1536))
    return np.concatenate(cols)


def kernel(x_prompt, x_sample, cache_k, cache_v, cache_conv, norm1_g, w_in, b_in, attn_sinks,
           conv_w, conv_b, conv_ln_g, conv_ln_b, w_out, b_out, norm2_g, w_up, w_down, final_norm_g):
    f32 = np.float32
    inputs = dict(norm1_g=norm1_g, norm2_g=norm2_g, conv_b=conv_b, conv_ln_g=conv_ln_g, conv_ln_b=conv_ln_b,
                  conv_w=conv_w)
    inputs = {k: np.asarray(v, f32) for k, v in inputs.items()}
    cst = _host_consts(inputs)
    perm = _col_perm()
    w_in_p = np.ascontiguousarray(np.asarray(w_in, f32)[0][:, perm])
    b_in_p = np.asarray(b_in, f32)[0][perm]
    cst[:, C_BFM:C_BFM + 26] = b_in_p[0:26 * 128].reshape(26, 128).T
    sinks = np.asarray(attn_sinks, f32)[0]
    sk = np.array([sinks[head_of(c, b)] for c in range(8) for b in range(2)], f32)
    cst[:, C_SK:C_SK + 16] = sk[None, :]
    cst[:, C_EPS] = EPS
    bvv = np.ascontiguousarray(b_in_p[26 * 128:26 * 128 + 256])
    rows = []
    for c in range(8):
        for b in range(2):
            h = head_of(c, b)
            rows.append(np.arange(h * 64, (h + 1) * 64))
    rows.append(np.arange(1024, 2048))
    rows = np.concatenate(rows)
    w_out_p = np.ascontiguousarray(np.asarray(w_out, f32)[0][rows, :])
    w_up_ = np.asarray(w_up, f32)[0]
    w_down_ = np.asarray(w_down, f32)[0]

    def tile_of(W, r0, c0):
        return W[r0:r0 + 2048, c0:c0 + 256].reshape(16, 128, 256).transpose(1, 0, 2).reshape(128, 4096)

    wt = np.empty((86, 128, 4096), f32)
    ti_ = 0
    for t in range(14):
        wt[ti_] = tile_of(w_in_p, 0, t * 256); ti_ += 1
    for d in range(8):
        wt[ti_] = tile_of(w_out_p, 0, d * 256); ti_ += 1
    for q in range(4):
        for t in range(8):
            wt[ti_] = tile_of(w_up_, 0, q * 2048 + t * 256); ti_ += 1
        for d in range(8):
            wt[ti_] = tile_of(w_down_, q * 2048, d * 256); ti_ += 1
    half = 8
    inv_freq = np.power(f32(500000.0), -np.arange(half, dtype=f32) * f32(2.0) / f32(16)).astype(f32)

    def tables(pos):
        ang = pos.astype(f32)[:, None] * inv_freq[None, :]
        co = np.cos(ang).astype(f32).T
        si = np.sin(ang).astype(f32).T
        T = pos.shape[0]
        tab = np.zeros((128, 2, T), f32)
        tab[:, 0, :] = 1.0
        for base in (0, 64):
            tab[base:base + 8, 0, :] = co
            tab[base + 8:base + 16, 0, :] = co
            tab[base:base + 8, 1, :] = -si
            tab[base + 8:base + 16, 1, :] = si
        return tab

    csp = tables(np.arange(SEQ, dtype=np.int32))
    css = tables(np.full((NS,), 16384, dtype=np.int32))

    if "nc" not in _NC_CACHE:
        _NC_CACHE["nc"] = build_nc()
    nc = _NC_CACHE["nc"]

    xpa = np.asarray(x_prompt, f32)
    xsa = np.asarray(x_sample, f32)
    cka = np.asarray(cache_k, f32)[0].reshape(128, 128, 256)
    cva = np.asarray(cache_v, f32)[0].reshape(128, 128, 256)
    cca = np.asarray(cache_conv, f32)[0]
    gfv = np.ascontiguousarray(np.asarray(final_norm_g, f32))
    bov = np.ascontiguousarray(np.asarray(b_out, f32)[0])
    in_maps = []
    for c in range(NCORE):
        in_maps.append({
            "xp": np.ascontiguousarray(xpa[2 * c:2 * c + 2].reshape(TOK, D)),
            "xs": np.ascontiguousarray(xsa[NS * c:NS * (c + 1), 0, :]),
            "ck": np.ascontiguousarray(cka[NS * c:NS * (c + 1)]),
            "cv": np.ascontiguousarray(cva[NS * c:NS * (c + 1)]),
            "cc": np.ascontiguousarray(cca[NS * c:NS * (c + 1)]),
            "wt": wt,
            "cst": cst, "csp": csp, "css": css, "gf": gfv, "bo": bov, "bv": bvv,
        })
    res = run_bass_kernel_spmd(nc, in_maps, core_ids=list(range(NCORE)))
    rs = res.results
    y_p = np.concatenate([r["y_p"].reshape(2, SEQ, D) for r in rs], axis=0)
    y_s = np.concatenate([r["y_s"].reshape(NS, 1, D) for r in rs], axis=0)
    pk = np.concatenate([r["o_pk"].reshape(2, 128, 4, 64) for r in rs], axis=0)[None]
    pv = np.concatenate([r["o_pv"].reshape(2, 128, 4, 64) for r in rs], axis=0)[None]
    pc = np.concatenate([r["o_pc"].reshape(2, 30, 1024) for r in rs], axis=0)[None]
    sk_ = np.concatenate([r["o_sk"].reshape(NS, 128, 4, 64) for r in rs], axis=0)[None]
    sv_ = np.concatenate([r["o_sv"].reshape(NS, 128, 4, 64) for r in rs], axis=0)[None]
    sc_ = np.concatenate([r["o_sc"].reshape(NS, 30, 1024) for r in rs], axis=0)[None]
    return (y_p, y_s, pk, pv, pc, sk_, sv_, sc_)
```
